# Optimizing a Trainium2 kernel written in Bass

```python
import math
import jax, jax.numpy as jnp
from jax import lax
import numpy as np

D_MODEL = 1024
BATCH = 32
SEQ = 2048
DEPTH = 2
DEC_BATCH = 16
DEC_SEQ = 2048
PAST_LEN = 128

GLA_HEADS = 4
GLA_DK = 64
GLA_DV = 128
GLA_QK_W = GLA_HEADS * GLA_DK
GLA_V_W = GLA_HEADS * GLA_DV
GATE_RANK = 16
GATE_TAU = 16.0
CHUNK = 64
NAT_HEADS = 8
NAT_HD = 64
NAT_W = NAT_HEADS * NAT_HD
GRID_W = 64
MAX_KR = 8
KC = 16
QB = 16
KB = 32
PROJ_SIZES = (GLA_QK_W, GLA_QK_W, GLA_V_W, GLA_V_W, 2 * GATE_RANK, NAT_W, NAT_W, NAT_W)
PROJ_W = 3104
MIX_W = GLA_V_W + NAT_W
FFN_HIDDEN = ((8 * D_MODEL + 3 * 256 - 1) // (3 * 256)) * 256
ALPHA = (2 * DEPTH) ** 0.25
BETA = (8 * DEPTH) ** -0.25
LN_EPS = 1e-5

kernel_name = "hybrid_gla_natten_deepnorm_encoder"


def _layer_norm(x, g, b):
    xf = x.astype(jnp.float32)
    mu = jnp.mean(xf, axis=-1, keepdims=True)
    var = jnp.mean(jnp.square(xf - mu), axis=-1, keepdims=True)
    y = (xf - mu) * lax.rsqrt(var + LN_EPS) * g.astype(jnp.float32) + b.astype(jnp.float32)
    return y.astype(x.dtype)


def _gla_chunked(q, k, v, g, include_diag):
    B, H, L, dk = q.shape
    dv = v.shape[-1]
    n = L // CHUNK
    f32 = jnp.float32
    q = q.astype(f32).reshape(B, H, n, CHUNK, dk)
    k = k.astype(f32).reshape(B, H, n, CHUNK, dk)
    v = v.astype(f32).reshape(B, H, n, CHUNK, dv)
    g = g.astype(f32).reshape(B, H, n, CHUNK, dk)
    bcum = jnp.cumsum(g, axis=3)
    b_last = bcum[:, :, :, -1:, :]
    q_dec = q * jnp.exp(bcum)
    k_inv = k * jnp.exp(-bcum)
    k_end = k * jnp.exp(b_last - bcum)
    mask = jnp.tril(jnp.ones((CHUNK, CHUNK), dtype=bool), k=0 if include_diag else -1)
    scores = jnp.einsum('bhncd,bhnsd->bhncs', q_dec, k_inv)
    intra = jnp.einsum('bhncs,bhnsv->bhncv', jnp.where(mask, scores, 0.0), v)
    dS = jnp.einsum('bhnsd,bhnsv->bhndv', k_end, v)
    decay = jnp.exp(b_last[:, :, :, 0, :])

    def step(S, inp):
        dec, ds = inp
        return dec[..., None] * S + ds, S

    S0 = jnp.zeros((B, H, dk, dv), f32)
    _, S_before = lax.scan(step, S0, (jnp.moveaxis(decay, 2, 0), jnp.moveaxis(dS, 2, 0)))
    S_before = jnp.moveaxis(S_before, 0, 2)
    inter = jnp.einsum('bhncd,bhndv->bhncv', q_dec, S_before)
    return (intra + inter).reshape(B, H, L, dv)


def _nat_col_tables():
    nqb = GRID_W // QB
    qcols = np.arange(GRID_W).reshape(nqb, QB)
    col_start = np.clip(qcols - KC // 2, 0, GRID_W - KC)
    blk_start = np.clip(np.arange(nqb) * QB - KC // 2, 0, GRID_W - KB)
    key_cols = blk_start[:, None] + np.arange(KB)
    kc3 = key_cols[:, None, :]
    valid = (kc3 >= col_start[:, :, None]) & (kc3 < col_start[:, :, None] + KC)
    co_idx = np.clip(kc3 - qcols[:, :, None] + KC - 1, 0, 2 * KC - 2)
    return (jnp.asarray(key_cols, jnp.int32), jnp.asarray(valid), jnp.asarray(co_idx, jnp.int32))


def _neighborhood_attention(q, k, v, rpb):
    B, L, H, d = q.shape
    rows = L // GRID_W
    kr = min(MAX_KR, rows)
    nqb = GRID_W // QB
    key_cols, valid, co_idx = _nat_col_tables()
    qg = (q * (d ** -0.5)).reshape(B, rows, GRID_W, H, d)
    kg = k.reshape(B, rows, GRID_W, H, d)
    vg = v.reshape(B, rows, GRID_W, H, d)

    def row_fn(r):
        rs = jnp.clip(r - kr // 2, 0, rows - kr)
        k_rows = lax.dynamic_slice_in_dim(kg, rs, kr, axis=1)
        v_rows = lax.dynamic_slice_in_dim(vg, rs, kr, axis=1)
        k_blk = k_rows[:, :, key_cols]
        v_blk = v_rows[:, :, key_cols]
        q_row = lax.dynamic_index_in_dim(qg, r, axis=1, keepdims=False).reshape(B, nqb, QB, H, d)
        s = jnp.einsum('bjqhd,bkjmhd->bhjqkm', q_row, k_blk).astype(jnp.float32)
        ro = rs + jnp.arange(kr) - r + (MAX_KR - 1)
        bias = rpb[:, ro[:, None, None, None], co_idx[None]]
        s = s + jnp.transpose(bias, (0, 2, 3, 1, 4)).astype(jnp.float32)[None]
        s = jnp.where(valid[None, None, :, :, None, :], s, -1e30)
        p = jax.nn.softmax(s.reshape(B, H, nqb, QB, kr * KB), axis=-1).reshape(s.shape)
        o = jnp.einsum('bhjqkm,bkjmhd->bjqhd', p.astype(v.dtype), v_blk)
        return o.reshape(B, GRID_W, H, d)

    out = lax.map(row_fn, jnp.arange(rows))
    return jnp.transpose(out, (1, 0, 2, 3, 4)).reshape(B, L, H * d)


def _token_mixer(x, w_in, gate_w2, gate_b, gla_norm_g, rpb, w_out):
    B, L, _ = x.shape
    f32 = jnp.float32
    proj = x @ w_in
    points = [int(p) for p in np.cumsum(PROJ_SIZES)[:-1]]
    q_a, k_a, v_a, r_a, lr, q_b, k_b, v_b = jnp.split(proj, points, axis=-1)

    def heads(t, h, dh):
        return t.reshape(B, L, h, dh).transpose(0, 2, 1, 3)

    qa = heads(q_a, GLA_HEADS, GLA_DK) * (GLA_DK ** -0.5)
    ka = heads(k_a, GLA_HEADS, GLA_DK)
    va = heads(v_a, GLA_HEADS, GLA_DV)
    lr = lr.astype(f32)
    g_f = jax.nn.log_sigmoid(lr[..., :GATE_RANK] @ gate_w2[0].astype(f32) + gate_b[0].astype(f32)) / GATE_TAU
    g_b = jax.nn.log_sigmoid(lr[..., GATE_RANK:] @ gate_w2[1].astype(f32) + gate_b[1].astype(f32)) / GATE_TAU
    g_f = heads(g_f, GLA_HEADS, GLA_DK)
    g_b = heads(g_b, GLA_HEADS, GLA_DK)
    o_fwd = _gla_chunked(qa, ka, va, g_f, True)
    o_bwd = _gla_chunked(qa[:, :, ::-1], ka[:, :, ::-1], va[:, :, ::-1], g_b[:, :, ::-1], False)[:, :, ::-1]
    o = (o_fwd + o_bwd).transpose(0, 2, 1, 3)
    o = o * lax.rsqrt(jnp.mean(jnp.square(o), axis=-1, keepdims=True) + LN_EPS)
    o = o * gla_norm_g.astype(f32).reshape(GLA_HEADS, GLA_DV)
    o_a = (o.reshape(B, L, GLA_V_W) * jax.nn.silu(r_a.astype(f32))).astype(x.dtype)

    o_n = _neighborhood_attention(q_b.reshape(B, L, NAT_HEADS, NAT_HD),
                                  k_b.reshape(B, L, NAT_HEADS, NAT_HD),
                                  v_b.reshape(B, L, NAT_HEADS, NAT_HD), rpb)
    return jnp.concatenate([o_a, o_n.astype(x.dtype)], axis=-1) @ w_out


def _swiglu(x, w_ffn_in, w_ffn_out):
    h = x @ w_ffn_in
    gate, up = jnp.split(h, 2, axis=-1)
    return (jax.nn.silu(gate) * up) @ w_ffn_out


def _trunk(x, w_in, gla_gate_w2, gla_gate_b, gla_norm_g, nat_rpb, w_out,
           ln1_g, ln1_b, w_ffn_in, w_ffn_out, ln2_g, ln2_b):
    for l in range(DEPTH):
        mix = _token_mixer(x, w_in[l], gla_gate_w2[l], gla_gate_b[l], gla_norm_g[l], nat_rpb[l], w_out[l])
        x = _layer_norm(ALPHA * x + mix, ln1_g[l], ln1_b[l])
        x = _layer_norm(ALPHA * x + _swiglu(x, w_ffn_in[l], w_ffn_out[l]), ln2_g[l], ln2_b[l])
    return x


def setup_inputs(seed: int = 0) -> dict:
    key = jax.random.key(seed)
    ks = jax.random.split(key, 16)
    f32 = jnp.float32
    nrm = lambda k, s: jax.random.normal(k, s, f32)
    return {
        "x_prompt": nrm(ks[0], (BATCH, SEQ, D_MODEL)),
        "x_sample": nrm(ks[1], (DEC_BATCH, DEC_SEQ, D_MODEL)),
        "w_in": nrm(ks[2], (DEPTH, D_MODEL, PROJ_W)) * D_MODEL ** -0.5,
        "gla_gate_w2": nrm(ks[3], (DEPTH, 2, GATE_RANK, GLA_QK_W)) * GATE_RANK ** -0.5,
        "gla_gate_b": nrm(ks[4], (DEPTH, 2, GLA_QK_W)) * 0.1,
        "gla_norm_g": 1.0 + 0.01 * nrm(ks[5], (DEPTH, GLA_V_W)),
        "nat_rpb": 0.02 * nrm(ks[6], (DEPTH, NAT_HEADS, 2 * MAX_KR - 1, 2 * KC - 1)),
        "w_out": nrm(ks[7], (DEPTH, MIX_W, D_MODEL)) * (MIX_W ** -0.5) * BETA,
        "ln1_g": 1.0 + 0.01 * nrm(ks[8], (DEPTH, D_MODEL)),
        "ln1_b": 0.01 * nrm(ks[9], (DEPTH, D_MODEL)),
        "w_ffn_in": nrm(ks[10], (DEPTH, D_MODEL, 2 * FFN_HIDDEN)) * D_MODEL ** -0.5,
        "w_ffn_out": nrm(ks[11], (DEPTH, FFN_HIDDEN, D_MODEL)) * (FFN_HIDDEN ** -0.5) * BETA,
        "ln2_g": 1.0 + 0.01 * nrm(ks[12], (DEPTH, D_MODEL)),
        "ln2_b": 0.01 * nrm(ks[13], (DEPTH, D_MODEL)),
    }


def reference(x_prompt, x_sample, w_in, gla_gate_w2, gla_gate_b, gla_norm_g, nat_rpb, w_out,
              ln1_g, ln1_b, w_ffn_in, w_ffn_out, ln2_g, ln2_b):
    y_prompt = _trunk(x_prompt, w_in, gla_gate_w2, gla_gate_b, gla_norm_g, nat_rpb, w_out,
                      ln1_g, ln1_b, w_ffn_in, w_ffn_out, ln2_g, ln2_b)
    y_sample = _trunk(x_sample, w_in, gla_gate_w2, gla_gate_b, gla_norm_g, nat_rpb, w_out,
                      ln1_g, ln1_b, w_ffn_in, w_ffn_out, ln2_g, ln2_b)
    return (y_prompt, y_sample)
```

```python
import numpy as np
from contextlib import ExitStack
import concourse.bass as bass
import concourse.mybir as mybir
from concourse.bass_utils import run_bass_kernel_spmd

F32 = mybir.dt.float32
BF16 = mybir.dt.bfloat16
AF = mybir.ActivationFunctionType
ALU = mybir.AluOpType

COMPUTE = ("pe", "act", "dve", "pool")
import os as _os
_MODEL_NOWAR = bool(_os.environ.get("MODEL_NOWAR"))
_MODEL_DROP = tuple(x for x in _os.environ.get("MODEL_DROP", "").split(",") if x)
_MODEL_KEEP = tuple(_os.environ.get("MODEL_KEEP", "XT,xs,x1s").split(","))


class Res:
    __slots__ = ("name", "w", "r", "excl", "alias", "lo", "hi")

    def __init__(self, name, excl=False):
        self.name = name
        self.w = None
        self.r = {}
        self.excl = excl
        self.alias = ()
        self.lo = 0
        self.hi = 0


class Op:
    __slots__ = ("eng", "dom", "fn", "deps", "odeps", "signal", "count", "is_dma", "cost", "lat", "idx", "done", "nsucc", "pos")

    def __init__(self, eng, dom, fn, is_dma, cost, lat):
        self.eng = eng
        self.dom = dom
        self.fn = fn
        self.deps = ()
        self.odeps = ()
        self.signal = False
        self.count = 0
        self.is_dma = is_dma
        self.cost = cost
        self.lat = lat
        self.idx = 0
        self.done = -1.0


class _Stop(Exception):
    pass


def _fsize(ap):
    n = 1
    for d in ap.shape[1:]:
        n *= int(d)
    return n


class _AttachEng:
    def __init__(self, eng, sem, val):
        self._e = eng
        self._sem = sem
        self._val = val
        self._done = False

    def __getattr__(self, name):
        f = getattr(self._e, name)

        def g(*a, **kw):
            r = f(*a, **kw)
            if not self._done:
                r._wait_ge(self._sem, self._val)
                self._done = True
            return r
        return g


class _FakeIns:
    def then_inc(self, *a, **k):
        return self

    def _wait_ge(self, *a, **k):
        return self


class _FakeEng:
    def __init__(self, kind):
        self.kind = kind
        self.total = 0.0

    def matmul(self, out, lhsT=None, rhs=None, **kw):
        n = _fsize(rhs)
        f = 4.0 if rhs.dtype == F32 else 1.0
        self.total += f * max(n, 64) / 2.4 + 10.0
        return _FakeIns()

    def transpose(self, out=None, in_=None, identity=None, **kw):
        self.total += 70.0
        return _FakeIns()

    def dma_start(self, out=None, in_=None, **kw):
        n = 1
        for d in in_.shape:
            n *= int(d)
        self.total += n * 4 / 250.0
        return _FakeIns()

    def __getattr__(self, name):
        def f(*a, **kw):
            ap = kw.get("out", None)
            if ap is None:
                ap = kw.get("ap", a[0] if a else None)
            n = _fsize(ap) if ap is not None else 64
            if self.kind == "act":
                self.total += 230.0 + n / 1.1
            elif self.kind == "dve":
                self.total += 110.0 + n / 0.9
            else:
                self.total += 350.0 + n / 0.6
            return _FakeIns()
        return f


class Prog:
    def __init__(self):
        self.ops = []
        self._pos = {}

    def _add(self, op, reads, writes):
        deps = {}
        odeps = {}
        pos = self._pos.get(op.eng, 0)
        self._pos[op.eng] = pos + 1
        op.pos = pos
        rd_rec, wr_rec = [], []
        for r in reads:
            (wr_rec if r.excl else rd_rec).append(r)
        for w in writes:
            wr_rec.append(w)

        def need(d):
            if d is None:
                return
            if d.dom == op.dom and not op.is_dma and op.eng == "pe":
                odeps[id(d)] = d
                return
            deps[id(d)] = d

        for r in rd_rec:
            need(r.w)
            for a in r.alias:
                need(a.w)
        for w in wr_rec:
            for x in (w,) + tuple(w.alias):
                if _MODEL_NOWAR and not x.excl and not x.alias and not x.name.startswith(_MODEL_KEEP) and (not _MODEL_DROP or x.name.startswith(_MODEL_DROP)):
                    continue
                if op.is_dma and x.w is not None and x.w.is_dma and x.w.dom == op.dom:
                    odeps[id(x.w)] = x.w
                else:
                    need(x.w)
                for dl in x.r.values():
                    for d in dl:
                        need(d)
        op.deps = tuple(deps.values())
        op.odeps = tuple(odeps.values())
        for r in rd_rec:
            lst = r.r.setdefault(op.dom, [])
            lst.append(op)
            if op.is_dma:
                del lst[:-1]
            else:
                while len(lst) > 1 and lst[0].pos < pos - 96:
                    lst.pop(0)
        for w in wr_rec:
            w.w = op
            w.r = {}
        op.idx = len(self.ops)
        self.ops.append(op)
        return op

    def op(self, eng, fn, reads=(), writes=(), c=None):
        if c is None:
            fe = _FakeEng(eng)
            fn(fe)
            c = fe.total
        return self._add(Op(eng, eng, fn, False, c, 0.0), reads, writes)

    def dma(self, fn, reads=(), writes=(), q="sp", dom="d0"):
        fe = _FakeEng("dma")
        fn(fe)
        issue = 1000.0 if q == "pool" else 100.0
        return self._add(Op(q, dom, fn, True, issue, 2000.0 + fe.total), reads, writes)

    def schedule(self, window=48, hop=250.0):
        per_eng = {}
        for o in self.ops:
            per_eng.setdefault(o.eng, []).append(o)
        order = {e: [] for e in per_eng}
        pos = {e: 0 for e in per_eng}
        pending = {e: list(l) for e, l in per_eng.items()}
        free = {e: 0.0 for e in per_eng}
        nleft = len(self.ops)
        while nleft:
            best = None
            for e, lst in pending.items():
                if not lst:
                    continue
                lim = window
                seen_dma = set()
                k = 0
                for o in lst:
                    if k >= lim:
                        break
                    k += 1
                    if o.is_dma:
                        if o.dom in seen_dma:
                            continue
                        seen_dma.add(o.dom)
                    rdy = 0.0
                    ok = True
                    for d in o.deps:
                        if d.done < 0:
                            ok = False
                            break
                        t = d.done + (hop if d.eng != e or d.is_dma else 60.0)
                        if t > rdy:
                            rdy = t
                    if ok:
                        for d in o.odeps:
                            if d.done < 0:
                                ok = False
                                break
                    if not ok:
                        continue
                    st = rdy if rdy > free[e] else free[e]
                    key = (st, o.idx)
                    if best is None or key < best[0]:
                        best = (key, e, o)
                    if rdy <= free[e]:
                        break
            if best is None:
                raise RuntimeError("scheduler deadlock")
            (st, _), e, o = best
            free[e] = st + o.cost
            o.done = st + o.cost + o.lat
            pending[e].remove(o)
            order[e].append(o)
            nleft -= 1
        self.est_ns = max(o.done for o in self.ops)
        return order

    def emit(self, block, sems, final_eng="sp", reorder=True):
        if reorder:
            order = self.schedule()
        else:
            order = {}
            for o in self.ops:
                order.setdefault(o.eng, []).append(o)
        for o in self.ops:
            for d in o.deps:
                d.signal = True
        cnt = {}
        for e, lst in order.items():
            for o in lst:
                if o.is_dma:
                    o.signal = True
                    cnt[o.dom] = cnt.get(o.dom, 0) + 16
                    o.count = cnt[o.dom]
                elif o.signal:
                    cnt[o.dom] = cnt.get(o.dom, 0) + 1
                    o.count = cnt[o.dom]
        final_counts = dict(cnt)
        engs = {"pe": "tensor", "act": "scalar", "dve": "vector", "pool": "gpsimd", "sp": "sync"}

        def make(olist, is_final):
            def body(e):
                seen = {}
                for o in olist:
                    need = {}
                    for d in o.deps:
                        if seen.get(d.dom, 0) >= d.count:
                            continue
                        if need.get(d.dom, 0) < d.count:
                            need[d.dom] = d.count
                    items = list(need.items())
                    for dom, c in items:
                        seen[dom] = c
                    att = None
                    if items and not o.is_dma:
                        att = items.pop()
                    for dom, c in items:
                        e.wait_ge(sems[dom], c)
                    if att is not None:
                        ins = o.fn(_AttachEng(e, sems[att[0]], att[1]))
                    else:
                        ins = o.fn(e)
                    if o.signal:
                        ins.then_inc(sems[o.dom], 16 if o.is_dma else 1)
                if is_final:
                    for dom, c in final_counts.items():
                        if seen.get(dom, 0) < c:
                            e.wait_ge(sems[dom], c)
            return body

        for ename in ("sp", "act", "pool", "dve", "pe"):
            olist = order.get(ename, [])
            is_final = ename == final_eng
            if not olist and not is_final:
                continue
            getattr(block, engs[ename])(make(olist, is_final))
        self.counts = final_counts


L = 2048
D = 1024
NT = 16
NL = 2
NCORES = 8
NSEQ = 6
FH = 2816
NJ = 22
ALPHA = float((2 * NL) ** 0.25)
EPS = 1e-5
MASKV = -30000.0
C_QA, C_KA, C_VA, C_RA, C_LR, C_QB, C_KB, C_VB = 0, 256, 512, 1024, 1536, 1568, 2080, 2592
NCST = 7 * 128 + 2


def _host_consts():
    s = np.arange(128)[:, None]
    t = np.arange(128)[None, :]
    same = (s // 64) == (t // 64)
    g = -1.0 / 16.0
    ident = np.eye(128, dtype=np.float32)
    triF = np.where(same & (s <= t), g, 0.0)
    triB = np.where(same & (s >= t), g, 0.0)
    triSF = np.where(same & (s > t), g, 0.0)
    triSB = np.where(same & (s < t), g, 0.0)
    mF = np.where(same & (s <= t), 1.0, 0.0)
    mB = np.where(same & (s > t), 1.0, 0.0)
    cind = np.where((np.arange(128)[:, None] // 64) == np.arange(2)[None, :], g, 0.0)
    return np.concatenate([ident, triF, triB, triSF, triSB, mF, mB, cind], axis=1).astype(np.float32)


def _nat_index_tables():
    p = np.arange(128)[:, None, None]
    j = np.arange(19)[None, :, None]
    q = np.arange(64)[None, None, :]
    kcol = p % 64
    half = p // 64
    cs = np.clip(q - 8, 0, 48)
    colok = (kcol >= cs) & (kcol < cs + 16)
    co = np.clip(kcol - q + 15, 0, 30)
    ro_even = j + half
    cc = j - 14
    kk = 2 * cc + half - 1
    ro_odd = kk + 3
    is_odd = j >= 14
    ro = np.where(is_odd, ro_odd, ro_even)
    rowok = np.where(is_odd, (kk >= 0) & (kk <= 7), ro_even <= 14)
    valid = colok & rowok
    ro = np.clip(ro, 0, 14)
    ro = np.broadcast_to(ro, (128, 19, 64))
    co = np.broadcast_to(co, (128, 19, 64))
    valid = np.broadcast_to(valid, (128, 19, 64))
    return ro, co, valid


def build(ns=NSEQ, nl=NL, taps=None):
    taps = taps or {}
    nc = bass.Bass("TRN2", target_bir_lowering=False)
    dram_in = lambda n, s: nc.dram_tensor(n, s, F32, kind="ExternalInput").ap()
    x_d = dram_in("x", [ns, L, D])
    w_in_d = dram_in("w_in", [NL, D, 3104])
    w_out_d = dram_in("w_out", [NL, D, D])
    w1_d = dram_in("w1", [NL, D, 2 * FH])
    w2_d = dram_in("w2", [NL, FH, D])
    w2g_d = dram_in("w2g", [NL, 33, 512])
    gn_d = dram_in("gn", [NL, 128, 512])
    lnp_d = dram_in("lnp", [NL, 4, 128, D])
    natb_d = dram_in("natb", [NL, 4, 128, 2 * 19 * 64])
    natm_d = dram_in("natm", [128, 19 * 64])
    cst_d = dram_in("cst", [128, NCST])
    y_d = nc.dram_tensor("y", [ns, L, D], F32, kind="ExternalOutput").ap()
    xs_d = nc.dram_tensor("xs_scr", [L, D], F32).ap()
    x1s_d = nc.dram_tensor("x1s_scr", [L, D], F32).ap()
    tap_d = {k: nc.dram_tensor(k, list(shp), F32, kind="ExternalOutput").ap() for k, shp in taps.items() if not k.startswith("_")}

    es = ExitStack()
    with es:
        sb = lambda n, s, d=F32: es.enter_context(nc.sbuf_tensor(n, s, d))
        psb = lambda n: es.enter_context(nc.psum_tensor(n, [128, 512], F32))
        arA = sb("arA", [128, 22656], BF16)
        arB = sb("arB", [128, 22784], BF16)
        XT = sb("XT", [128, 8, L], BF16)
        w1b = [sb(f"w1b{i}", [128, 8, 256], BF16) for i in range(2)]
        lnt = sb("lnt", [128, 2, D])
        natT = sb("natT", [128, 2, 19, 64])
        natM = sb("natM", [128, 19, 64])
        w2g = sb("w2g_sb", [33, 512])
        gnt = sb("gnt", [128, 512])
        cst = sb("cst_sb", [128, NCST])
        identb = sb("identb", [128, 128], BF16)
        lrT = sb("lrT", [33, 512], BF16)
        w2gb = sb("w2gb", [33, 512], BF16)
        qraw = sb("qraw", [128, 512])
        kraw = sb("kraw", [128, 512])
        ktm = sb("ktm", [128, 128])
        vbf = sb("vbf", [128, 256], BF16)
        rsb = sb("rsb", [128, 256])
        t1 = sb("t1", [128, 256])
        sp_ = sb("sp", [128, 256])
        EP = sb("EP", [128, 2, 128])
        EN = sb("EN", [128, 2, 128])
        EE = sb("EE", [128, 128])
        decb = sb("decb", [128, 2])
        qd = sb("qd", [128, 2, 128], BF16)
        ki = sb("ki", [128, 2, 128], BF16)
        ke = sb("ke", [128, 128], BF16)
        Asb = sb("Asb", [128, 256])
        Bsb = sb("Bsb", [128, 256])
        PT = sb("PT", [128, 2, 128], BF16)
        S32 = [sb(f"S32_{i}", [128, 128]) for i in range(2)]
        Sbf = [sb(f"Sbf_{i}", [128, 128], BF16) for i in range(2)]
        GR = sb("GR", [128, 256])
        ssq = sb("ssq", [128, 2])
        rstd = sb("rstd", [128, 2])
        junk = sb("junk", [128, 128])
        oa = sb("oa", [128, 256], BF16)
        Ssb = [sb(f"Ssb{i}", [128, 5, 64]) for i in range(4)]
        Pbf = [sb(f"Pbf{i}", [128, 5, 64], BF16) for i in range(4)]
        rcp = [sb(f"rcp{i}", [128, 2]) for i in range(2)]
        onb = [sb(f"onb{i}", [128, 128], BF16) for i in range(2)]
        xr = [sb(f"xr{i}", [128, D]) for i in range(3)]
        bst = sb("bst", [128, 2, 6])
        mv = sb("mv", [128, 2])
        lnr = sb("lnr", [128, 2])
        nmr = sb("nmr", [128, 1])
        xhb = [sb(f"xhb{i}", [128, D], BF16) for i in range(2)]
        sg = [sb(f"sg{i}", [128, 512]) for i in range(2)]
        PS = [psb(f"ps{i}") for i in range(8)]
        PSb = [p[:].bitcast(BF16) for p in PS]
        BK = [Res(f"bank{i}", excl=True) for i in range(8)]

        P = Prog()

        mixT = arA[:, 0:16384].rearrange("p (k t) -> p k t", k=8)
        qbT = arA[:, 16384:18432]
        kbT = arA[:, 18432:20480]
        vbA = arA[:, 20480:22560].rearrange("p (t h e) -> p t h e", t=16, h=2)
        act = arA[:, 0:22528].rearrange("p (j t) -> p j t", j=NJ)
        woutT = arB[:, 0:8192].rearrange("p (k c) -> p k c", k=8)
        wGf = arB[:, 8192:10496].rearrange("p (k c) -> p k c", k=8)
        wGt = arB[:, 10496:15616].rearrange("p (k c) -> p k c", k=8)
        wN = arB[:, 15616:18688].rearrange("p (k c) -> p k c", k=8)
        SbB = arB[:, 18688:22784].rearrange("p (n v) -> p n v", n=32)
        w2T = arB[:, 0:22528].rearrange("p (j c) -> p j c", j=NJ)

        class _TSet:
            pass
        _tspec = [("ktm", 128, F32), ("vbf", 256, BF16), ("rsb", 256, F32), ("t1", 256, F32), ("sp_", 256, F32), ("EP", 256, F32),
                  ("EN", 256, F32), ("EE", 128, F32), ("decb", 2, F32), ("qd", 256, BF16), ("ki", 256, BF16), ("ke", 128, BF16),
                  ("Asb", 256, F32), ("Bsb", 256, F32), ("PT", 256, BF16), ("GR", 256, F32), ("ssq", 2, F32), ("rstd", 2, F32),
                  ("junk", 128, F32), ("oa", 256, BF16)]
        _t0 = {"ktm": ktm, "vbf": vbf, "rsb": rsb, "t1": t1, "sp_": sp_, "EP": EP, "EN": EN, "EE": EE, "decb": decb, "qd": qd, "ki": ki,
               "ke": ke, "Asb": Asb, "Bsb": Bsb, "PT": PT, "GR": GR, "ssq": ssq, "rstd": rstd, "junk": junk, "oa": oa}
        TB = [_TSet(), _TSet()]
        TB[0].i, TB[1].i = 0, 1
        _off = 0
        _tres = {}
        for (nm_, n_, dt_) in _tspec:
            setattr(TB[0], nm_, _t0[nm_])
            w_ = n_ * (2 if dt_ == F32 else 1)
            v_ = arB[:, _off:_off + w_]
            if dt_ == F32:
                v_ = v_.bitcast(F32)
            if nm_ in ("EP", "EN", "qd", "ki", "PT"):
                v_ = v_.rearrange("p (d t) -> p d t", d=2)
            setattr(TB[1], nm_, v_)
            _tres[nm_] = (_off, _off + w_)
            _off += w_ + (w_ % 2)
        assert _off <= 8192

        RES = {}
        arena_lists = {"A": [], "B": []}

        def R(name, key=0, arena=None, lo=0, hi=0):
            k = (name, key)
            r = RES.get(k)
            if r is None:
                r = Res(f"{name}{key}")
                RES[k] = r
                if arena is not None:
                    r.lo, r.hi = lo, hi
                    al = []
                    for o in arena_lists[arena]:
                        if o.lo < hi and lo < o.hi:
                            al.append(o)
                            o.alias = tuple(o.alias) + (r,)
                    r.alias = tuple(al)
                    arena_lists[arena].append(r)
            return r

        def Rmix(kc, t):
            return R("mixT", (kc, t), "A", kc * 2048 + t * 128, kc * 2048 + t * 128 + 128)

        def Rqb(t):
            return R("qbT", t, "A", 16384 + t * 128, 16384 + t * 128 + 128)

        def Rkb(t):
            return R("kbT", t, "A", 18432 + t * 128, 18432 + t * 128 + 128)

        def Rvb(t):
            return R("vbA", t, "A", 20480 + t * 130, 20480 + t * 130 + 130)

        def Ract(j, blk):
            return R("act", (j, blk), "A", j * 1024 + blk * 512, j * 1024 + blk * 512 + 512)

        Rwout = lambda: R("wout", 0, "B", 0, 8192)
        RwGf = lambda: R("wGf", 0, "B", 8192, 10496)
        RwGt = lambda: R("wGt", 0, "B", 10496, 15616)
        RwN = lambda: R("wN", 0, "B", 15616, 18688)
        RSb = lambda n: R("SbB", n, "B", 18688 + n * 128, 18688 + n * 128 + 128)
        Rw2 = lambda j: R("w2T", j, "B", j * 1024, j * 1024 + 1024)
        RXT = lambda t: R("XT", t)
        _rn = {"sp": "sp_"}
        TB[0].R = lambda name: R(name + "_s0")
        TB[1].R = lambda name: R(name + "_s1", 0, "B", *_tres[_rn.get(name, name)])
        for kc in range(8):
            for t in range(NT):
                Rmix(kc, t)
        for t in range(NT):
            Rqb(t), Rkb(t), Rvb(t)
        for j in range(NJ):
            Ract(j, 0), Ract(j, 1), Rw2(j)
        Rwout(), RwGf(), RwGt(), RwN()
        for n in range(32):
            RSb(n)

        DSF = (PS[6][:, 0:128], PS[5][:, 384:512])
        c_ident = cst[:, 0:128]
        c_triF = cst[:, 128:256]
        c_triB = cst[:, 256:384]
        c_triSF = cst[:, 384:512]
        c_triSB = cst[:, 512:640]
        c_mF = cst[:, 640:768]
        c_mB = cst[:, 768:896]
        c_cind = cst[:, 896:898]

        P.dma(lambda e: e.dma_start(out=cst[:], in_=cst_d[:, :]), writes=[R("cst")], dom="l_cst")
        P.dma(lambda e: e.dma_start(out=natM[:].rearrange("p j q -> p (j q)"), in_=natm_d[:, :]), writes=[R("natM")], dom="l_natM")
        P.op("dve", lambda e: e.tensor_copy(out=identb[:], in_=c_ident), reads=[R("cst")], writes=[R("identb")])
        P.op("pool", lambda e: e.memset(lrT[32:33, :], 1.0), writes=[R("lrT1")])

        def tap(name, src_ap, res, dst=None):
            if name in tap_d:
                d = tap_d[name] if dst is None else dst
                P.dma(lambda e: e.dma_start(out=d, in_=src_ap), reads=res, dom="s_tap")

        def phase_x0(s):
            for t in reversed(range(NT)):
                b = t % 2
                P.dma(lambda e, t=t, b=b: e.dma_start(out=xhb[b][:], in_=x_d[s, t * 128:(t + 1) * 128, :]),
                      writes=[R("xhb", b)], q="pool", dom=f"l_xhb{b}")
                transposes_to_XT(xhb[b], R("xhb", b), t, 6 + b, None)

        def transposes_to_XT(src_bf, src_res, t, bank, scale_bias):
            def tr(e):
                ins = None
                for kc in range(8):
                    ins = e.transpose(out=PSb[bank][:, kc * 128:(kc + 1) * 128], in_=src_bf[:, kc * 128:(kc + 1) * 128],
                                      identity=identb[:])
                return ins
            P.op("pe", tr, reads=[src_res, R("identb")], writes=[BK[bank]])
            dst = XT[:, :, t * 128:(t + 1) * 128]
            src = PSb[bank][:, 0:1024].rearrange("p (k c) -> p k c", k=8)
            P.op("dve", lambda e: e.tensor_copy(out=dst, in_=src), reads=[BK[bank]], writes=[RXT(t)])

        def load_layer_tables(l):
            P.dma(lambda e: e.dma_start(out=w2g[:], in_=w2g_d[l, :, :]), writes=[R("w2g")], dom="l_w2g")
            P.op("dve", lambda e: e.tensor_copy(out=w2gb[:], in_=w2g[:]), reads=[R("w2g")], writes=[R("w2gb")])
            P.dma(lambda e: e.dma_start(out=gnt[:], in_=gn_d[l, :, :]), writes=[R("gnt")], dom="l_gnt")

        def gla_pass(s, l, p):
            wl = w_in_d[l].rearrange("(k q) c -> q k c", q=128)
            for (dst, c0, n) in ((wGf[:, :, 0:128], C_QA + 128 * p, 128), (wGf[:, :, 128:256], C_KA + 128 * p, 128),
                                 (wGf[:, :, 256:288], C_LR, 32)):
                P.dma(lambda e, dst=dst, c0=c0, n=n: e.dma_start(out=dst, in_=wl[:, :, c0:c0 + n]),
                      writes=[RwGf()], q="pool", dom="l_wGf")
            for (dst, c0, n) in ((wGt[:, :, 0:128], C_KA + 128 * p, 128), (wGt[:, :, 128:384], C_VA + 256 * p, 256),
                                 (wGt[:, :, 384:640], C_RA + 256 * p, 256)):
                P.dma(lambda e, dst=dst, c0=c0, n=n: e.dma_start(out=dst, in_=wl[:, :, c0:c0 + n]),
                      writes=[RwGt()], q="pool", dom="l_wGt")
            gf0 = 128 * p
            gb0 = 256 + 128 * p

            def proj_fm(blk, c0, n, bank, m0=0):
                def f(e):
                    ins = None
                    for kc in range(8):
                        ins = e.matmul(PS[bank][m0:m0 + n, 0:512], lhsT=wGf[:, kc, c0:c0 + n],
                                       rhs=XT[:, kc, blk * 512:(blk + 1) * 512], start=(kc == 0), stop=(kc == 7))
                    return ins
                P.op("pe", f, reads=[RwGf()] + [RXT(4 * blk + i) for i in range(4)], writes=[BK[bank]])

            def proj_tm(t, c0, n, bank):
                def f(e):
                    ins = None
                    for kc in range(8):
                        ins = e.matmul(PS[bank][:, 0:n], lhsT=XT[:, kc, t * 128:(t + 1) * 128], rhs=wGt[:, kc, c0:c0 + n],
                                       start=(kc == 0), stop=(kc == 7))
                    return ins
                P.op("pe", f, reads=[RwGt(), RXT(t)], writes=[BK[bank]])

            def lr_block(blk):
                proj_fm(blk, 256, 32, 2)
                P.op("act", lambda e: e.copy(out=lrT[0:32, :], in_=PS[2][0:32, 0:512]), reads=[BK[2]], writes=[R("lrT")])

            def softplus_neg(ncols, B):
                P.op("act", lambda e: e.activation(out=B.t1[:, 0:ncols], in_=PS[2][:, 0:ncols], func=AF.Exp, scale=-1.0),
                     reads=[BK[2]], writes=[B.R("t1")])
                P.op("act", lambda e: e.activation(out=B.sp_[:, 0:ncols], in_=B.t1[:, 0:ncols], func=AF.Ln, bias=1.0),
                     reads=[B.R("t1")], writes=[B.R("sp")])

            def tile_a(t, ti, blk, cur, B):
                t = 4 * blk + ti
                proj_tm(t, 0, 384, 3)
                P.op("act", lambda e: e.copy(out=B.ktm[:], in_=PS[3][:, 0:128]), reads=[BK[3]], writes=[B.R("ktm")])
                P.op("dve", lambda e: e.tensor_copy(out=B.vbf[:], in_=PS[3][:, 128:384]), reads=[BK[3]], writes=[B.R("vbf")])
                ck("A2")
                P.op("pe", lambda e, ti=ti: e.matmul(PS[2][:, 0:128], lhsT=lrT[0:33, ti * 128:(ti + 1) * 128],
                                                     rhs=w2gb[0:33, gb0:gb0 + 128], start=True, stop=True),
                     reads=[R("lrT"), R("lrT1"), R("w2gb")], writes=[BK[2]])
                softplus_neg(128, B)
                ck("A3")

                def cums(e):
                    e.matmul(PS[5][:, 0:128], lhsT=c_triSB, rhs=B.sp_[:, 0:128], start=True, stop=True)
                    return e.matmul(PS[5][:, 128:130], lhsT=B.sp_[:, 0:128], rhs=c_cind, start=True, stop=True)
                P.op("pe", cums, reads=[B.R("sp"), R("cst")], writes=[BK[5]])
                P.op("act", lambda e: e.activation(out=B.EE[:], in_=PS[5][:, 0:128], func=AF.Exp), reads=[BK[5]], writes=[B.R("EE")])
                P.op("act", lambda e: e.activation(out=B.decb[:], in_=PS[5][:, 128:130], func=AF.Exp), reads=[BK[5]], writes=[B.R("decb")])
                P.op("dve", lambda e: e.tensor_tensor(out=B.ke[:], in0=B.ktm[:], in1=B.EE[:], op=ALU.mult),
                     reads=[B.R("ktm"), B.R("EE")], writes=[B.R("ke")])

                def dsb(e):
                    ins = None
                    for j in range(2):
                        for hh in range(2):
                            ins = e.matmul(PS[6 + j][64 * hh:64 * hh + 64, 0:128],
                                           lhsT=B.ke[64 * j:64 * j + 64, 64 * hh:64 * hh + 64],
                                           rhs=B.vbf[64 * j:64 * j + 64, 128 * hh:128 * hh + 128],
                                           start=True, stop=True, tile_position=(64 * j, 64 * hh))
                    return ins
                ck("A4")
                P.op("pe", dsb, reads=[B.R("ke"), B.R("vbf")], writes=[BK[6], BK[7]])
                ck("A5")
                for j in (1, 0):
                    n = 2 * t + j
                    P.op("pool", lambda e, n=n, cur=cur: e.tensor_copy(out=SbB[:, n, :], in_=S32[cur][:]),
                         reads=[R("S32", cur)], writes=[RSb(n)])
                    P.op("dve", lambda e, j=j, cur=cur: e.scalar_tensor_tensor(
                        out=S32[1 - cur][:], in0=S32[cur][:], scalar=B.decb[:, j:j + 1], in1=PS[6 + j][:, 0:128],
                        op0=ALU.mult, op1=ALU.add),
                        reads=[R("S32", cur), B.R("decb"), BK[6 + j]], writes=[R("S32", 1 - cur)])
                    cur = 1 - cur
                return cur

            def tile_b(t, ti, blk, cur, B):
                t = 4 * blk + ti
                tc0 = ti * 128
                proj_tm(t, 0, 384, 3)
                proj_tm(t, 384, 256, 4)
                P.op("act", lambda e: e.copy(out=B.ktm[:], in_=PS[3][:, 0:128]), reads=[BK[3]], writes=[B.R("ktm")])
                P.op("dve", lambda e: e.tensor_copy(out=B.vbf[:], in_=PS[3][:, 128:384]), reads=[BK[3]], writes=[B.R("vbf")])
                P.op("act", lambda e: e.activation(out=B.rsb[:], in_=PS[4][:, 0:256], func=AF.Exp, scale=-1.0), reads=[BK[4]], writes=[B.R("rsb")])
                P.op("act", lambda e: e.activation(out=B.rsb[:], in_=B.rsb[:], func=AF.Ln, bias=1.0), reads=[B.R("rsb")], writes=[B.R("rsb")])
                P.op("act", lambda e: e.activation(out=B.rsb[:], in_=B.rsb[:], func=AF.Exp, scale=-1.0), reads=[B.R("rsb")], writes=[B.R("rsb")])
                P.op("pool", lambda e: e.tensor_tensor(out=B.rsb[:], in0=B.rsb[:], in1=gnt[:, 256 * p:256 * p + 256], op=ALU.mult),
                     reads=[B.R("rsb"), R("gnt")], writes=[B.R("rsb")])
                P.op("dve", lambda e: e.tensor_tensor(out=B.GR[:], in0=PS[4][:, 0:256], in1=B.rsb[:], op=ALU.mult),
                     reads=[BK[4], B.R("rsb")], writes=[B.R("GR")])

                def zmm(e, tc0=tc0):
                    e.matmul(PS[2][:, 0:128], lhsT=lrT[0:33, tc0:tc0 + 128], rhs=w2gb[0:33, gf0:gf0 + 128], start=True, stop=True)
                    return e.matmul(PS[2][:, 128:256], lhsT=lrT[0:33, tc0:tc0 + 128], rhs=w2gb[0:33, gb0:gb0 + 128], start=True, stop=True)
                P.op("pe", zmm, reads=[R("lrT"), R("lrT1"), R("w2gb")], writes=[BK[2]])
                softplus_neg(256, B)

                def cums2(e):
                    e.matmul(PS[5][:, 0:128], lhsT=B.sp_[:, 0:128], rhs=c_triF, start=True, stop=True)
                    e.matmul(PS[5][:, 128:256], lhsT=B.sp_[:, 128:256], rhs=c_triB, start=True, stop=True)
                    return e.matmul(PS[5][:, 256:384], lhsT=c_triSF, rhs=B.sp_[:, 0:128], start=True, stop=True)
                P.op("pe", cums2, reads=[B.R("sp"), R("cst")], writes=[BK[5]])
                bc = PS[5][:, 0:256].rearrange("p (d t) -> p d t", d=2)
                P.op("act", lambda e, bc=bc: e.activation(out=B.EP[:], in_=bc, func=AF.Exp), reads=[BK[5]], writes=[B.R("EP")])
                P.op("act", lambda e, bc=bc: e.activation(out=B.EN[:], in_=bc, func=AF.Exp, scale=-1.0), reads=[BK[5]], writes=[B.R("EN")])
                P.op("act", lambda e: e.activation(out=B.EE[:], in_=PS[5][:, 256:384], func=AF.Exp), reads=[BK[5]], writes=[B.R("EE")])
                for d in range(2):
                    P.op("dve", lambda e, d=d, tc0=tc0: e.scalar_tensor_tensor(
                        out=B.qd[:, d, :], in0=qraw[:, tc0:tc0 + 128], scalar=0.125, in1=B.EP[:, d, :], op0=ALU.mult, op1=ALU.mult),
                        reads=[R("qraw"), B.R("EP")], writes=[B.R("qd")])
                    P.op("dve", lambda e, d=d, tc0=tc0: e.tensor_tensor(out=B.ki[:, d, :], in0=kraw[:, tc0:tc0 + 128], in1=B.EN[:, d, :], op=ALU.mult),
                         reads=[R("kraw"), B.R("EN")], writes=[B.R("ki")])
                P.op("dve", lambda e: e.tensor_tensor(out=B.ke[:], in0=B.ktm[:], in1=B.EE[:], op=ALU.mult),
                     reads=[B.R("ktm"), B.R("EE")], writes=[B.R("ke")])

                def scores(e):
                    ins = None
                    for d in range(2):
                        for hh in range(2):
                            ins = e.matmul(PS[hh][:, d * 128:(d + 1) * 128],
                                           lhsT=B.ki[64 * hh:64 * hh + 64, d, :], rhs=B.qd[64 * hh:64 * hh + 64, d, :],
                                           start=True, stop=True, tile_position=(64 * hh, 0))
                    return ins
                P.op("pe", scores, reads=[B.R("ki"), B.R("qd")], writes=[BK[0], BK[1]])
                for hh in range(2):
                    P.op("dve", lambda e, hh=hh: e.tensor_tensor(out=B.Asb[:, hh * 128:(hh + 1) * 128], in0=PS[hh][:, 0:128],
                                                              in1=c_mF, op=ALU.mult), reads=[BK[hh], R("cst")], writes=[B.R("Asb")])
                    P.op("dve", lambda e, hh=hh: e.tensor_tensor(out=B.Bsb[:, hh * 128:(hh + 1) * 128], in0=PS[hh][:, 128:256],
                                                              in1=c_mB, op=ALU.mult), reads=[BK[hh], R("cst")], writes=[B.R("Bsb")])
                P.op("pool", lambda e: e.tensor_tensor(out=B.PT[:].rearrange("p h c -> p (h c)"), in0=B.Asb[:], in1=B.Bsb[:], op=ALU.add),
                     reads=[B.R("Asb"), B.R("Bsb")], writes=[B.R("PT")])

                def dsf(e):
                    ins = None
                    for j in range(2):
                        for hh in range(2):
                            ins = e.matmul(DSF[j][64 * hh:64 * hh + 64, :],
                                           lhsT=B.ke[64 * j:64 * j + 64, 64 * hh:64 * hh + 64],
                                           rhs=B.vbf[64 * j:64 * j + 64, 128 * hh:128 * hh + 128],
                                           start=True, stop=True, tile_position=(64 * j, 64 * hh))
                    return ins
                P.op("pe", dsf, reads=[B.R("ke"), B.R("vbf")], writes=[BK[6], BK[5]])
                c0 = cur
                for j in range(2):
                    P.op("dve", lambda e, j=j, cur=cur: e.scalar_tensor_tensor(
                        out=S32[1 - cur][:], in0=S32[cur][:], scalar=B.EP[:, 0, 64 * j + 63:64 * j + 64],
                        in1=DSF[j][:, :], op0=ALU.mult, op1=ALU.add),
                        reads=[R("S32", cur), B.R("EP"), BK[(6, 5)[j]]], writes=[R("S32", 1 - cur)])
                    cur = 1 - cur
                    if j == 0:
                        P.op("pool", lambda e, cur=cur: e.tensor_copy(out=Sbf[1][:], in_=S32[cur][:]), reads=[R("S32", cur)], writes=[R("Sbf", 1)])
                def omm(e, t=t):
                    ins = None
                    for hh in range(2):
                        o_ = PS[7][:, hh * 128:(hh + 1) * 128]
                        e.matmul(o_, lhsT=B.PT[:, hh, :], rhs=B.vbf[:, 128 * hh:128 * hh + 128], start=True, stop=False)
                        for j in range(2):
                            n = 2 * t + j
                            oj = PS[7][64 * j:64 * j + 64, hh * 128:(hh + 1) * 128]
                            e.matmul(oj, lhsT=B.qd[64 * hh:64 * hh + 64, 0, 64 * j:64 * j + 64], rhs=Sbf[j][64 * hh:64 * hh + 64, :],
                                     start=False, stop=False, tile_position=(64 * hh, 64 * j))
                            ins = e.matmul(oj, lhsT=B.qd[64 * hh:64 * hh + 64, 1, 64 * j:64 * j + 64], rhs=SbB[64 * hh:64 * hh + 64, n, :],
                                           start=False, stop=True, tile_position=(64 * hh, 64 * j))
                    return ins
                P.op("pe", omm, reads=[B.R("PT"), B.R("vbf"), B.R("qd"), R("Sbf", 0), R("Sbf", 1), RSb(2 * t), RSb(2 * t + 1)], writes=[BK[7]])
                P.op("pool", lambda e, cur=cur: e.tensor_copy(out=Sbf[0][:], in_=S32[cur][:]), reads=[R("S32", cur)], writes=[R("Sbf", 0)])
                for hh in range(2):
                    P.op("act", lambda e, hh=hh: e.activation(out=B.junk[:], in_=PS[7][:, hh * 128:(hh + 1) * 128], func=AF.Square,
                                                             accum_out=B.ssq[:, hh:hh + 1]), reads=[BK[7]], writes=[B.R("ssq"), B.R("junk")])
                P.op("act", lambda e: e.activation(out=B.rstd[:], in_=B.ssq[:], func=AF.Ln, scale=1.0 / 128.0, bias=EPS), reads=[B.R("ssq")], writes=[B.R("rstd")])
                P.op("act", lambda e: e.activation(out=B.rstd[:], in_=B.rstd[:], func=AF.Exp, scale=-0.5), reads=[B.R("rstd")], writes=[B.R("rstd")])
                for hh in range(2):
                    P.op("dve", lambda e, hh=hh: e.scalar_tensor_tensor(
                        out=B.oa[:, hh * 128:(hh + 1) * 128], in0=PS[7][:, hh * 128:(hh + 1) * 128], scalar=B.rstd[:, hh:hh + 1],
                        in1=B.GR[:, hh * 128:(hh + 1) * 128], op0=ALU.mult, op1=ALU.mult),
                        reads=[BK[7], B.R("rstd"), B.R("GR")], writes=[B.R("oa")])

                def otr(e):
                    e.transpose(out=PSb[7][:, 512:640], in_=B.oa[:, 0:128], identity=identb[:])
                    return e.transpose(out=PSb[7][:, 640:768], in_=B.oa[:, 128:256], identity=identb[:])
                P.op("pe", otr, reads=[B.R("oa"), R("identb")], writes=[BK[7]])
                P.op("dve", lambda e, t=t: e.tensor_copy(out=mixT[:, 2 * p:2 * p + 2, t * 128:(t + 1) * 128],
                                                       in_=PSb[7][:, 512:768].rearrange("p (k c) -> p k c", k=2)),
                     reads=[BK[7]], writes=[Rmix(2 * p, t), Rmix(2 * p + 1, t)])
                return cur

            P.op("pool", lambda e: e.memset(S32[0][:], 0.0), writes=[R("S32", 0)])
            cur = 0
            gs = taps.get("_gstop")

            def ck(name):
                if gs == name:
                    raise _Stop()
            for blk in (3, 2, 1, 0):
                ck("A0")
                lr_block(blk)
                ck("A1")
                for ti in (3, 2, 1, 0):
                    cur = tile_a(4 * blk + ti, ti, blk, cur, TB[ti % 2])
            if taps.get("_gstop") == "A":
                return
            P.op("pool", lambda e: e.memset(S32[0][:], 0.0), writes=[R("S32", 0)])
            P.op("pool", lambda e: e.memset(Sbf[0][:], 0.0), writes=[R("Sbf", 0)])
            cur = 0
            for blk in range(4):
                proj_fm(blk, 0, 128, 0)
                proj_fm(blk, 128, 128, 1)
                lr_block(blk)
                P.op("act", lambda e: e.copy(out=qraw[:], in_=PS[0][:, :]), reads=[BK[0]], writes=[R("qraw")])
                P.op("dve", lambda e: e.tensor_copy(out=kraw[:], in_=PS[1][:, :]), reads=[BK[1]], writes=[R("kraw")])
                for ti in range(4):
                    cur = tile_b(4 * blk + ti, ti, blk, cur, TB[ti % 2])

        def nat_pass(s, l, c):
            wl = w_in_d[l].rearrange("(k q) c -> q k c", q=128)
            for (dst, c0) in ((wN[:, :, 0:128], C_QB + 128 * c), (wN[:, :, 128:256], C_KB + 128 * c), (wN[:, :, 256:384], C_VB + 128 * c)):
                P.dma(lambda e, dst=dst, c0=c0: e.dma_start(out=dst, in_=wl[:, :, c0:c0 + 128]), writes=[RwN()], q="pool", dom="l_wN")
            P.dma(lambda e: e.dma_start(out=natT[:].rearrange("p h j q -> p (h j q)"), in_=natb_d[l, c, :, :]), writes=[R("natT")], dom="l_natT")
            for hh in range(2):
                P.op("pool", lambda e, hh=hh: e.tensor_tensor(out=natT[:, hh, :, :], in0=natT[:, hh, :, :], in1=natM[:], op=ALU.add),
                     reads=[R("natT"), R("natM")], writes=[R("natT")])
            for i in range(2):
                P.op("pool", lambda e, i=i: e.memset(vbA[:, :, i, 64:65], 1.0), writes=[Rvb(t) for t in range(NT)])
            for blk in range(4):
                for (c0, bank) in ((0, 0), (128, 1)):
                    def f(e, c0=c0, bank=bank, blk=blk):
                        ins = None
                        for kc in range(8):
                            ins = e.matmul(PS[bank][:, 0:512], lhsT=wN[:, kc, c0:c0 + 128], rhs=XT[:, kc, blk * 512:(blk + 1) * 512],
                                           start=(kc == 0), stop=(kc == 7))
                        return ins
                    P.op("pe", f, reads=[RwN()] + [RXT(4 * blk + i) for i in range(4)], writes=[BK[bank]])
                P.op("act", lambda e, blk=blk: e.mul(out=qbT[:, blk * 512:(blk + 1) * 512], in_=PS[0][:, :], mul=0.125),
                     reads=[BK[0]], writes=[Rqb(4 * blk + i) for i in range(4)])
                P.op("dve", lambda e, blk=blk: e.tensor_copy(out=kbT[:, blk * 512:(blk + 1) * 512], in_=PS[1][:, :]),
                     reads=[BK[1]], writes=[Rkb(4 * blk + i) for i in range(4)])
                for ti in range(4):
                    t = 4 * blk + ti
                    vbk = (2, 7, 3, 4)[ti]

                    def fv(e, t=t, vbk=vbk):
                        ins = None
                        for kc in range(8):
                            ins = e.matmul(PS[vbk][:, 0:128], lhsT=XT[:, kc, t * 128:(t + 1) * 128], rhs=wN[:, kc, 256:384],
                                           start=(kc == 0), stop=(kc == 7))
                        return ins
                    P.op("pe", fv, reads=[RwN(), RXT(t)], writes=[BK[vbk]])
                    P.op("act" if ti % 2 == 0 else "dve", (lambda e, t=t, vbk=vbk: e.copy(out=vbA[:, t, :, 0:64], in_=PS[vbk][:, 0:128].rearrange("p (h d) -> p h d", h=2)))
                         if ti % 2 == 0 else (lambda e, t=t, vbk=vbk: e.tensor_copy(out=vbA[:, t, :, 0:64], in_=PS[vbk][:, 0:128].rearrange("p (h d) -> p h d", h=2))),
                         reads=[BK[vbk]], writes=[Rvb(t)])
            it = 0
            for t in range(NT):
                for rr in range(2):
                    r = 2 * t + rr
                    rs = min(max(r - 4, 0), 24)
                    if rs % 2 == 0:
                        a0, nch = rs // 2, 4
                        jb = [(rs - r + 7) + 2 * cc for cc in range(4)]
                        tab = lambda hh, jb=jb: natT[:, hh, jb[0]:jb[0] + 7:2, :]
                    else:
                        a0, nch = (rs - 1) // 2, 5
                        tab = lambda hh: natT[:, hh, 14:19, :]
                    for hh in range(2):
                        bank = 3 + (it % 4)
                        sbi = it % 4
                        it += 1

                        def smm(e, a0=a0, nch=nch, hh=hh, bank=bank, r=r):
                            ins = None
                            for cc in range(nch):
                                a = a0 + cc
                                ins = e.matmul(PS[bank][:, cc * 64:(cc + 1) * 64], lhsT=kbT[64 * hh:64 * hh + 64, a * 128:(a + 1) * 128],
                                               rhs=qbT[64 * hh:64 * hh + 64, r * 64:(r + 1) * 64], start=True, stop=True)
                            return ins
                        P.op("pe", smm, reads=[Rkb(a0 + cc) for cc in range(nch)] + [Rqb(t)], writes=[BK[bank]])
                        psv = PS[bank][:, 0:nch * 64].rearrange("p (c q) -> p c q", c=nch)
                        P.op("dve", lambda e, psv=psv, nch=nch, sbi=sbi, tab=tab, hh=hh: e.tensor_tensor(
                            out=Ssb[sbi][:, 0:nch, :], in0=psv, in1=tab(hh), op=ALU.add),
                            reads=[BK[bank], R("natT")], writes=[R("Ssb", sbi)])
                        P.op("act", lambda e, nch=nch, sbi=sbi: e.activation(out=Pbf[sbi][:, 0:nch, :], in_=Ssb[sbi][:, 0:nch, :], func=AF.Exp),
                             reads=[R("Ssb", sbi)], writes=[R("Pbf", sbi)])

                        def pv(e, a0=a0, nch=nch, hh=hh, sbi=sbi, rr=rr):
                            ins = None
                            for cc in range(nch):
                                a = a0 + cc
                                ins = e.matmul(PS[7][64 * rr:64 * rr + 64, hh * 65:hh * 65 + 65], lhsT=Pbf[sbi][:, cc, :],
                                               rhs=vbA[:, a, hh, :], start=(cc == 0), stop=(cc == nch - 1), tile_position=(0, 64 * rr))
                            return ins
                        P.op("pe", pv, reads=[R("Pbf", sbi)] + [Rvb(a0 + cc) for cc in range(nch)], writes=[BK[7]])
                ov = PS[7][:, 0:130].rearrange("p (h e) -> p h e", h=2)
                tb_ = t % 2
                P.op("dve", lambda e, ov=ov, tb_=tb_: e.reciprocal(out=rcp[tb_][:].rearrange("p (h o) -> p h o", o=1), in_=ov[:, :, 64:65]),
                     reads=[BK[7]], writes=[R("rcp", tb_)])
                for hh in range(2):
                    P.op("dve", lambda e, hh=hh, tb_=tb_: e.tensor_scalar(out=onb[tb_][:, hh * 64:(hh + 1) * 64], in0=PS[7][:, hh * 65:hh * 65 + 64],
                                                               scalar1=rcp[tb_][:, hh:hh + 1], scalar2=None, op0=ALU.mult),
                         reads=[BK[7], R("rcp", tb_)], writes=[R("onb", tb_)])
                P.op("pe", lambda e, tb_=tb_: e.transpose(out=PSb[2][:, 512:640], in_=onb[tb_][:], identity=identb[:]), reads=[R("onb", tb_), R("identb")], writes=[BK[2]])
                P.op("act", lambda e, t=t: e.copy(out=mixT[:, 4 + c, t * 128:(t + 1) * 128], in_=PSb[2][:, 512:640]),
                     reads=[BK[2]], writes=[Rmix(4 + c, t)])

        def layer_norm_tile(buf, bres, gi):
            xb = xr[buf]

            def st(e):
                e.bn_stats(out=bst[:, 0, :], in_=xb[:, 0:512])
                return e.bn_stats(out=bst[:, 1, :], in_=xb[:, 512:1024])
            P.op("dve", st, reads=[bres], writes=[R("bst")])
            P.op("dve", lambda e: e.bn_aggr(out=mv[:], in_=bst[:]), reads=[R("bst")], writes=[R("mv")])
            P.op("act", lambda e: e.activation(out=lnr[:, 0:1], in_=mv[:, 1:2], func=AF.Ln, bias=EPS), reads=[R("mv")], writes=[R("lnr")])
            P.op("act", lambda e: e.activation(out=lnr[:, 1:2], in_=lnr[:, 0:1], func=AF.Exp, scale=-0.5), reads=[R("lnr")], writes=[R("lnr")])
            P.op("dve", lambda e: e.scalar_tensor_tensor(out=xb[:], in0=xb[:], scalar=mv[:, 0:1], in1=lnt[:, 0, :], op0=ALU.subtract, op1=ALU.mult),
                 reads=[bres, R("mv"), R("lnt")], writes=[bres])
            P.op("act", lambda e: e.activation(out=xb[:], in_=xb[:], func=AF.Copy, scale=lnr[:, 1:2]), reads=[bres, R("lnr")], writes=[bres])
            P.op("pool", lambda e: e.tensor_tensor(out=xb[:], in0=xb[:], in1=lnt[:, 1, :], op=ALU.add), reads=[bres, R("lnt")], writes=[bres])

        def phase_o(s, l):
            wl = w_out_d[l].rearrange("(k q) c -> q k c", q=128)
            for h in range(2):
                P.dma(lambda e, h=h: e.dma_start(out=woutT[:, :, h * 512:(h + 1) * 512], in_=wl[:, :, h * 512:(h + 1) * 512]),
                      writes=[Rwout()], q="pool", dom="l_wout")
            P.dma(lambda e: e.dma_start(out=lnt[:], in_=lnp_d[l, 0:2].rearrange("g p d -> p g d")), writes=[R("lnt")], dom="l_lnt")
            src = x_d[s] if l == 0 else xs_d
            for t in range(NT):
                b = t % 3
                bres = R("xr", b)
                P.dma(lambda e, t=t, b=b: e.dma_start(out=xr[b][:], in_=src[t * 128:(t + 1) * 128, :]),
                      reads=([R("xs", t)] if l > 0 else []), writes=[bres], dom=f"l_xr{b}")
                bk = 2 * (t % 2)
                for h in range(2):
                    def f(e, h=h, t=t, bk=bk):
                        ins = None
                        for kc in range(8):
                            ins = e.matmul(PS[bk + h][:, :], lhsT=mixT[:, kc, t * 128:(t + 1) * 128], rhs=woutT[:, kc, h * 512:(h + 1) * 512],
                                           start=(kc == 0), stop=(kc == 7))
                        return ins
                    P.op("pe", f, reads=[Rwout()] + [Rmix(kc, t) for kc in range(8)], writes=[BK[bk + h]])
                    P.op("dve", lambda e, h=h, b=b, bk=bk: e.scalar_tensor_tensor(
                        out=xr[b][:, h * 512:(h + 1) * 512], in0=xr[b][:, h * 512:(h + 1) * 512], scalar=ALPHA, in1=PS[bk + h][:, :],
                        op0=ALU.mult, op1=ALU.add), reads=[bres, BK[bk + h]], writes=[bres])
                layer_norm_tile(b, bres, 0)
                P.dma(lambda e, t=t, b=b: e.dma_start(out=x1s_d[t * 128:(t + 1) * 128, :], in_=xr[b][:]), reads=[bres], writes=[R("x1s", t)], dom=f"s_xr{b}")
                hb = t % 2
                P.op("act", lambda e, b=b, hb=hb: e.copy(out=xhb[hb][:], in_=xr[b][:]), reads=[bres], writes=[R("xhb", hb)])
                transposes_to_XT(xhb[hb], R("xhb", hb), t, 4 + hb, None)

        def phase_f(s, l):
            w1l = w1_d[l].rearrange("(k q) c -> q k c", q=128)
            w2l = w2_d[l].rearrange("(j q) c -> q j c", q=128)
            for j0 in range(0, NJ, 6):
                j1 = min(NJ, j0 + 6)
                P.dma(lambda e, j0=j0, j1=j1: e.dma_start(out=w2T[:, j0:j1, :], in_=w2l[:, j0:j1, :]),
                      writes=[Rw2(j) for j in range(j0, j1)], q="pool", dom=f"l_w2_{j0}")
            P.dma(lambda e: e.dma_start(out=lnt[:], in_=lnp_d[l, 2:4].rearrange("g p d -> p g d")), writes=[R("lnt")], dom="l_lnt")
            it = 0
            for half in range(2):
                for j in range(NJ):
                    wb = j % 2
                    P.dma(lambda e, j=j, wb=wb: e.dma_start(out=w1b[wb][:, :, 0:128], in_=w1l[:, :, j * 128:(j + 1) * 128]),
                          writes=[R("w1b", wb)], q="pool", dom=f"l_w1b{wb}")
                    P.dma(lambda e, j=j, wb=wb: e.dma_start(out=w1b[wb][:, :, 128:256], in_=w1l[:, :, FH + j * 128:FH + (j + 1) * 128]),
                          writes=[R("w1b", wb)], q="pool", dom=f"l_w1b{wb}")
                    for bl in range(2):
                        tb = half * 2 + bl
                        gb, ub = (0, 1) if it % 2 == 0 else (2, 3)
                        sgi = it % 2
                        it += 1
                        for (bank, c0) in ((gb, 0), (ub, 128)):
                            def f(e, bank=bank, c0=c0, wb=wb, tb=tb):
                                ins = None
                                for kc in range(8):
                                    ins = e.matmul(PS[bank][:, :], lhsT=w1b[wb][:, kc, c0:c0 + 128], rhs=XT[:, kc, tb * 512:(tb + 1) * 512],
                                                   start=(kc == 0), stop=(kc == 7))
                                return ins
                            P.op("pe", f, reads=[R("w1b", wb)] + [RXT(4 * tb + i) for i in range(4)], writes=[BK[bank]])
                        P.op("act", lambda e, gb=gb, sgi=sgi: e.activation(out=sg[sgi][:], in_=PS[gb][:, :], func=AF.Silu),
                             reads=[BK[gb]], writes=[R("sg", sgi)])
                        P.op("dve", lambda e, ub=ub, sgi=sgi, j=j, bl=bl: e.tensor_tensor(
                            out=act[:, j, bl * 512:(bl + 1) * 512], in0=PS[ub][:, :], in1=sg[sgi][:], op=ALU.mult),
                            reads=[BK[ub], R("sg", sgi)], writes=[Ract(j, bl)])
                for tt in range(8):
                    t = half * 8 + tt
                    b = t % 3
                    bres = R("xr", b)
                    P.dma(lambda e, t=t, b=b: e.dma_start(out=xr[b][:], in_=x1s_d[t * 128:(t + 1) * 128, :]),
                          reads=[R("x1s", t)], writes=[bres], dom=f"l_xr{b}")
                    bk = 4 + 2 * (t % 2)
                    for h in range(2):
                        def f(e, h=h, tt=tt, bk=bk):
                            ins = None
                            for j in range(NJ):
                                ins = e.matmul(PS[bk + h][:, :], lhsT=act[:, j, tt * 128:(tt + 1) * 128], rhs=w2T[:, j, h * 512:(h + 1) * 512],
                                               start=(j == 0), stop=(j == NJ - 1))
                            return ins
                        P.op("pe", f, reads=[Rw2(j) for j in range(NJ)] + [Ract(j, tt // 4) for j in range(NJ)], writes=[BK[bk + h]])
                        P.op("dve", lambda e, h=h, b=b, bk=bk: e.scalar_tensor_tensor(
                            out=xr[b][:, h * 512:(h + 1) * 512], in0=xr[b][:, h * 512:(h + 1) * 512], scalar=ALPHA, in1=PS[bk + h][:, :],
                            op0=ALU.mult, op1=ALU.add), reads=[bres, BK[bk + h]], writes=[bres])
                    layer_norm_tile(b, bres, 2)
                    if l == nl - 1:
                        P.dma(lambda e, t=t, b=b: e.dma_start(out=y_d[s, t * 128:(t + 1) * 128, :], in_=xr[b][:]), reads=[bres], dom=f"s_xr{b}")
                    else:
                        P.dma(lambda e, t=t, b=b: e.dma_start(out=xs_d[t * 128:(t + 1) * 128, :], in_=xr[b][:]), reads=[bres],
                              writes=[R("xs", t)], dom=f"s_xr{b}")
                        hb = t % 2
                        P.op("act", lambda e, b=b, hb=hb: e.copy(out=xhb[hb][:], in_=xr[b][:]), reads=[bres], writes=[R("xhb", hb)])
                        transposes_to_XT(xhb[hb], R("xhb", hb), t, hb, None)

        phases = taps.get("_phases", ("x0", "gla", "nat", "o", "f"))
        for s in range(ns):
            if "x0" in phases:
                phase_x0(s)
            for l in range(nl):
                load_layer_tables(l)
                if "gla" in phases:
                    try:
                        for p in range(2):
                            gla_pass(s, l, p)
                    except _Stop:
                        pass
                if "nat" in phases:
                    for c in range(4):
                        nat_pass(s, l, c)
                if "mix" in tap_d and s == 0 and l == 0:
                    for kc in ([0, 1, 2, 3] if "gla" in phases else []) + ([4, 5, 6, 7] if "nat" in phases else []):
                        P.op("dve", lambda e, kc=kc: e.tensor_copy(out=xr[0][:, 0:1024], in_=mixT[:, kc, 0:1024]),
                             reads=[Rmix(kc, t) for t in range(8)], writes=[R("xr", 0)])
                        P.dma(lambda e, kc=kc: e.dma_start(out=tap_d["mix"][kc, :, 0:1024], in_=xr[0][:, 0:1024]), reads=[R("xr", 0)], dom="s_xr0")
                        P.op("dve", lambda e, kc=kc: e.tensor_copy(out=xr[0][:, 0:1024], in_=mixT[:, kc, 1024:2048]),
                             reads=[Rmix(kc, t) for t in range(8, 16)], writes=[R("xr", 0)])
                        P.dma(lambda e, kc=kc: e.dma_start(out=tap_d["mix"][kc, :, 1024:2048], in_=xr[0][:, 0:1024]), reads=[R("xr", 0)], dom="s_xr0")
                if "o" in phases:
                    phase_o(s, l)
                if "f" in phases:
                    phase_f(s, l)
        doms = ["pe", "act", "dve", "pool"]
        for o in P.ops:
            if o.dom not in doms:
                doms.append(o.dom)
        sems = {d: es.enter_context(nc.semaphore(d)) for d in doms}
        block = es.enter_context(nc.Block())
        P.emit(block, sems)
        nc._prog_stats = (len(P.ops), len(doms), getattr(P, 'est_ns', None))
    return nc


def _prep_shared(w_in, gla_gate_w2, gla_gate_b, gla_norm_g, nat_rpb, w_out, ln1_g, ln1_b, w_ffn_in, w_ffn_out, ln2_g, ln2_b):
    f = lambda a: np.ascontiguousarray(np.asarray(a, dtype=np.float32))
    w2g = np.zeros((NL, 33, 512), np.float32)
    w2g[:, 0:16, 0:256] = gla_gate_w2[:, 0]
    w2g[:, 16:32, 256:512] = gla_gate_w2[:, 1]
    w2g[:, 32, 0:256] = gla_gate_b[:, 0]
    w2g[:, 32, 256:512] = gla_gate_b[:, 1]
    gn = np.ascontiguousarray(np.broadcast_to(np.asarray(gla_norm_g, np.float32)[:, None, :], (NL, 128, 512)))
    lnp = np.stack([ln1_g, ln1_b, ln2_g, ln2_b], axis=1).astype(np.float32)
    lnp = np.ascontiguousarray(np.broadcast_to(lnp[:, :, None, :], (NL, 4, 128, D)))
    ro, co, valid = _nat_index_tables()
    rpb = np.asarray(nat_rpb, np.float32)
    g = rpb[:, :, ro, co]
    g = g.reshape(NL, 4, 2, 128, 19, 64).transpose(0, 1, 3, 2, 4, 5)
    natb = np.ascontiguousarray(g.reshape(NL, 4, 128, 2 * 19 * 64))
    natm = np.where(valid, np.float32(0.0), np.float32(MASKV)).astype(np.float32).reshape(128, 19 * 64)
    return {
        "w_in": f(w_in), "w_out": f(w_out), "w1": f(w_ffn_in), "w2": f(w_ffn_out),
        "w2g": w2g, "gn": gn, "lnp": lnp, "natb": natb, "natm": np.ascontiguousarray(natm), "cst": _host_consts(),
    }


_NC_CACHE = {}


def kernel(x_prompt, x_sample, w_in, gla_gate_w2, gla_gate_b, gla_norm_g, nat_rpb, w_out,
           ln1_g, ln1_b, w_ffn_in, w_ffn_out, ln2_g, ln2_b):
    xp = np.asarray(x_prompt, np.float32)
    xsm = np.asarray(x_sample, np.float32)
    shared = _prep_shared(np.asarray(w_in), np.asarray(gla_gate_w2), np.asarray(gla_gate_b), np.asarray(gla_norm_g),
                          np.asarray(nat_rpb), np.asarray(w_out), np.asarray(ln1_g), np.asarray(ln1_b),
                          np.asarray(w_ffn_in), np.asarray(w_ffn_out), np.asarray(ln2_g), np.asarray(ln2_b))
    in_maps = []
    for i in range(NCORES):
        xi = np.ascontiguousarray(np.concatenate([xp[4 * i:4 * i + 4], xsm[2 * i:2 * i + 2]], axis=0))
        m = {"x": xi}
        m.update(shared)
        in_maps.append(m)
    if "nc" not in _NC_CACHE:
        _NC_CACHE["nc"] = build()
    res = run_bass_kernel_spmd(_NC_CACHE["nc"], in_maps, core_ids=list(range(NCORES)))
    ys = [np.asarray(r["y"]) for r in res.results]
    y_prompt = np.concatenate([y[0:4] for y in ys], axis=0).astype(np.float32)
    y_sample = np.concatenate([y[4:6] for y in ys], axis=0).astype(np.float32)
    return (y_prompt, y_sample)
```

```python
import numpy as np
from contextlib import ExitStack
import concourse.bass as bass
import concourse.mybir as mybir
from concourse.bass_utils import run_bass_kernel_spmd

F32 = mybir.dt.float32
BF16 = mybir.dt.bfloat16
AF = mybir.ActivationFunctionType
ALU = mybir.AluOpType

COMPUTE = ("pe", "act", "dve", "pool")
import os as _os
_MODEL_NOWAR = bool(_os.environ.get("MODEL_NOWAR"))
_MODEL_DROP = tuple(x for x in _os.environ.get("MODEL_DROP", "").split(",") if x)
_MODEL_KEEP = tuple(_os.environ.get("MODEL_KEEP", "XT,xs,x1s").split(","))


class Res:
    __slots__ = ("name", "w", "r", "excl", "alias", "lo", "hi")

    def __init__(self, name, excl=False):
        self.name = name
        self.w = None
        self.r = {}
        self.excl = excl
        self.alias = ()
        self.lo = 0
        self.hi = 0


class Op:
    __slots__ = ("eng", "dom", "fn", "deps", "odeps", "signal", "count", "is_dma", "cost", "lat", "idx", "done", "nsucc", "pos")

    def __init__(self, eng, dom, fn, is_dma, cost, lat):
        self.eng = eng
        self.dom = dom
        self.fn = fn
        self.deps = ()
        self.odeps = ()
        self.signal = False
        self.count = 0
        self.is_dma = is_dma
        self.cost = cost
        self.lat = lat
        self.idx = 0
        self.done = -1.0


class _Stop(Exception):
    pass


def _fsize(ap):
    n = 1
    for d in ap.shape[1:]:
        n *= int(d)
    return n


class _AttachEng:
    def __init__(self, eng, sem, val):
        self._e = eng
        self._sem = sem
        self._val = val
        self._done = False

    def __getattr__(self, name):
        f = getattr(self._e, name)

        def g(*a, **kw):
            r = f(*a, **kw)
            if not self._done:
                r._wait_ge(self._sem, self._val)
                self._done = True
            return r
        return g


class _FakeIns:
    def then_inc(self, *a, **k):
        return self

    def _wait_ge(self, *a, **k):
        return self


class _FakeEng:
    def __init__(self, kind):
        self.kind = kind
        self.total = 0.0

    def matmul(self, out, lhsT=None, rhs=None, **kw):
        n = _fsize(rhs)
        f = 4.0 if rhs.dtype == F32 else 1.0
        self.total += f * max(n, 64) / 2.4 + 10.0
        return _FakeIns()

    def transpose(self, out=None, in_=None, identity=None, **kw):
        self.total += 70.0
        return _FakeIns()

    def dma_start(self, out=None, in_=None, **kw):
        n = 1
        for d in in_.shape:
            n *= int(d)
        self.total += n * 4 / 250.0
        return _FakeIns()

    def __getattr__(self, name):
        def f(*a, **kw):
            ap = kw.get("out", None)
            if ap is None:
                ap = kw.get("ap", a[0] if a else None)
            n = _fsize(ap) if ap is not None else 64
            if self.kind == "act":
                self.total += 230.0 + n / 1.1
            elif self.kind == "dve":
                self.total += 110.0 + n / 0.9
            else:
                self.total += 350.0 + n / 0.6
            return _FakeIns()
        return f


class Prog:
    def __init__(self):
        self.ops = []
        self._pos = {}

    def _add(self, op, reads, writes):
        deps = {}
        odeps = {}
        pos = self._pos.get(op.eng, 0)
        self._pos[op.eng] = pos + 1
        op.pos = pos
        rd_rec, wr_rec = [], []
        for r in reads:
            (wr_rec if r.excl else rd_rec).append(r)
        for w in writes:
            wr_rec.append(w)

        def need(d):
            if d is None:
                return
            if d.dom == op.dom and not op.is_dma and op.eng == "pe":
                odeps[id(d)] = d
                return
            deps[id(d)] = d

        for r in rd_rec:
            need(r.w)
            for a in r.alias:
                need(a.w)
        for w in wr_rec:
            for x in (w,) + tuple(w.alias):
                if _MODEL_NOWAR and not x.excl and not x.alias and not x.name.startswith(_MODEL_KEEP) and (not _MODEL_DROP or x.name.startswith(_MODEL_DROP)):
                    continue
                if op.is_dma and x.w is not None and x.w.is_dma and x.w.dom == op.dom:
                    odeps[id(x.w)] = x.w
                else:
                    need(x.w)
                for dl in x.r.values():
                    for d in dl:
                        need(d)
        op.deps = tuple(deps.values())
        op.odeps = tuple(odeps.values())
        for r in rd_rec:
            lst = r.r.setdefault(op.dom, [])
            lst.append(op)
            if op.is_dma:
                del lst[:-1]
            else:
                while len(lst) > 1 and lst[0].pos < pos - 96:
                    lst.pop(0)
        for w in wr_rec:
            w.w = op
            w.r = {}
        op.idx = len(self.ops)
        self.ops.append(op)
        return op

    def op(self, eng, fn, reads=(), writes=(), c=None):
        if c is None:
            fe = _FakeEng(eng)
            fn(fe)
            c = fe.total
        return self._add(Op(eng, eng, fn, False, c, 0.0), reads, writes)

    def dma(self, fn, reads=(), writes=(), q="sp", dom="d0"):
        fe = _FakeEng("dma")
        fn(fe)
        issue = 1000.0 if q == "pool" else 100.0
        return self._add(Op(q, dom, fn, True, issue, 2000.0 + fe.total), reads, writes)

    def schedule(self, window=48, hop=120.0):
        per_eng = {}
        for o in self.ops:
            per_eng.setdefault(o.eng, []).append(o)
        order = {e: [] for e in per_eng}
        pos = {e: 0 for e in per_eng}
        pending = {e: list(l) for e, l in per_eng.items()}
        free = {e: 0.0 for e in per_eng}
        nleft = len(self.ops)
        while nleft:
            best = None
            for e, lst in pending.items():
                if not lst:
                    continue
                lim = window
                seen_dma = set()
                k = 0
                for o in lst:
                    if k >= lim:
                        break
                    k += 1
                    if o.is_dma:
                        if o.dom in seen_dma:
                            continue
                        seen_dma.add(o.dom)
                    rdy = 0.0
                    ok = True
                    for d in o.deps:
                        if d.done < 0:
                            ok = False
                            break
                        t = d.done + (hop if d.eng != e or d.is_dma else 60.0)
                        if t > rdy:
                            rdy = t
                    if ok:
                        for d in o.odeps:
                            if d.done < 0:
                                ok = False
                                break
                    if not ok:
                        continue
                    st = rdy if rdy > free[e] else free[e]
                    key = (st, o.idx)
                    if best is None or key < best[0]:
                        best = (key, e, o)
                    if rdy <= free[e]:
                        break
            if best is None:
                raise RuntimeError("scheduler deadlock")
            (st, _), e, o = best
            free[e] = st + o.cost
            o.done = st + o.cost + o.lat
            pending[e].remove(o)
            order[e].append(o)
            nleft -= 1
        self.est_ns = max(o.done for o in self.ops)
        return order

    def emit(self, block, sems, final_eng="sp", reorder=True):
        if reorder:
            order = self.schedule()
        else:
            order = {}
            for o in self.ops:
                order.setdefault(o.eng, []).append(o)
        for o in self.ops:
            for d in o.deps:
                d.signal = True
        cnt = {}
        for e, lst in order.items():
            for o in lst:
                if o.is_dma:
                    o.signal = True
                    cnt[o.dom] = cnt.get(o.dom, 0) + 16
                    o.count = cnt[o.dom]
                elif o.signal:
                    cnt[o.dom] = cnt.get(o.dom, 0) + 1
                    o.count = cnt[o.dom]
        final_counts = dict(cnt)
        engs = {"pe": "tensor", "act": "scalar", "dve": "vector", "pool": "gpsimd", "sp": "sync"}

        def make(olist, is_final):
            def body(e):
                seen = {}
                for o in olist:
                    need = {}
                    for d in o.deps:
                        if seen.get(d.dom, 0) >= d.count:
                            continue
                        if need.get(d.dom, 0) < d.count:
                            need[d.dom] = d.count
                    items = list(need.items())
                    for dom, c in items:
                        seen[dom] = c
                    att = None
                    if items and not o.is_dma:
                        att = items.pop()
                    for dom, c in items:
                        e.wait_ge(sems[dom], c)
                    if att is not None:
                        ins = o.fn(_AttachEng(e, sems[att[0]], att[1]))
                    else:
                        ins = o.fn(e)
                    if o.signal:
                        ins.then_inc(sems[o.dom], 16 if o.is_dma else 1)
                if is_final:
                    for dom, c in final_counts.items():
                        if seen.get(dom, 0) < c:
                            e.wait_ge(sems[dom], c)
            return body

        for ename in ("sp", "act", "pool", "dve", "pe"):
            olist = order.get(ename, [])
            is_final = ename == final_eng
            if not olist and not is_final:
                continue
            getattr(block, engs[ename])(make(olist, is_final))
        self.counts = final_counts


L = 2048
D = 1024
NT = 16
NL = 2
NCORES = 8
NSEQ = 6
FH = 2816
NJ = 22
ALPHA = float((2 * NL) ** 0.25)
EPS = 1e-5
MASKV = -30000.0
C_QA, C_KA, C_VA, C_RA, C_LR, C_QB, C_KB, C_VB = 0, 256, 512, 1024, 1536, 1568, 2080, 2592
NCST = 7 * 128 + 2


def _host_consts():
    s = np.arange(128)[:, None]
    t = np.arange(128)[None, :]
    same = (s // 64) == (t // 64)
    g = -1.0 / 16.0
    ident = np.eye(128, dtype=np.float32)
    triF = np.where(same & (s <= t), g, 0.0)
    triB = np.where(same & (s >= t), g, 0.0)
    triSF = np.where(same & (s > t), g, 0.0)
    triSB = np.where(same & (s < t), g, 0.0)
    mF = np.where(same & (s <= t), 1.0, 0.0)
    mB = np.where(same & (s > t), 1.0, 0.0)
    cind = np.where((np.arange(128)[:, None] // 64) == np.arange(2)[None, :], g, 0.0)
    return np.concatenate([ident, triF, triB, triSF, triSB, mF, mB, cind], axis=1).astype(np.float32)


def _nat_index_tables():
    p = np.arange(128)[:, None, None]
    j = np.arange(19)[None, :, None]
    q = np.arange(64)[None, None, :]
    kcol = p % 64
    half = p // 64
    cs = np.clip(q - 8, 0, 48)
    colok = (kcol >= cs) & (kcol < cs + 16)
    co = np.clip(kcol - q + 15, 0, 30)
    ro_even = j + half
    cc = j - 14
    kk = 2 * cc + half - 1
    ro_odd = kk + 3
    is_odd = j >= 14
    ro = np.where(is_odd, ro_odd, ro_even)
    rowok = np.where(is_odd, (kk >= 0) & (kk <= 7), ro_even <= 14)
    valid = colok & rowok
    ro = np.clip(ro, 0, 14)
    ro = np.broadcast_to(ro, (128, 19, 64))
    co = np.broadcast_to(co, (128, 19, 64))
    valid = np.broadcast_to(valid, (128, 19, 64))
    return ro, co, valid


def build(ns=NSEQ, nl=NL, taps=None):
    taps = taps or {}
    nc = bass.Bass("TRN2", target_bir_lowering=False)
    dram_in = lambda n, s: nc.dram_tensor(n, s, F32, kind="ExternalInput").ap()
    x_d = dram_in("x", [ns, L, D])
    w_in_d = dram_in("w_in", [NL, D, 3104])
    w_out_d = dram_in("w_out", [NL, D, D])
    w1_d = dram_in("w1", [NL, D, 2 * FH])
    w2_d = dram_in("w2", [NL, FH, D])
    w2g_d = dram_in("w2g", [NL, 33, 512])
    gn_d = dram_in("gn", [NL, 128, 512])
    lnp_d = dram_in("lnp", [NL, 4, 128, D])
    natb_d = dram_in("natb", [NL, 4, 128, 2 * 19 * 64])
    natm_d = dram_in("natm", [128, 19 * 64])
    cst_d = dram_in("cst", [128, NCST])
    y_d = nc.dram_tensor("y", [ns, L, D], F32, kind="ExternalOutput").ap()
    xs_d = nc.dram_tensor("xs_scr", [L, D], F32).ap()
    x1s_d = nc.dram_tensor("x1s_scr", [L, D], F32).ap()
    tap_d = {k: nc.dram_tensor(k, list(shp), F32, kind="ExternalOutput").ap() for k, shp in taps.items() if not k.startswith("_")}

    es = ExitStack()
    with es:
        sb = lambda n, s, d=F32: es.enter_context(nc.sbuf_tensor(n, s, d))
        psb = lambda n: es.enter_context(nc.psum_tensor(n, [128, 512], F32))
        arA = sb("arA", [128, 22656], BF16)
        arB = sb("arB", [128, 22784], BF16)
        XT = sb("XT", [128, 8, L], BF16)
        w1b = [sb(f"w1b{i}", [128, 8, 256], BF16) for i in range(2)]
        lnt = sb("lnt", [128, 2, D])
        natT = sb("natT", [128, 2, 19, 64])
        natM = sb("natM", [128, 19, 64])
        w2g = sb("w2g_sb", [33, 512])
        gnt = sb("gnt", [128, 512])
        cst = sb("cst_sb", [128, NCST])
        identb = sb("identb", [128, 128], BF16)
        lrT = sb("lrT", [33, 512], BF16)
        w2gb = sb("w2gb", [33, 512], BF16)
        qraw = sb("qraw", [128, 512])
        kraw = sb("kraw", [128, 512])
        ktm = sb("ktm", [128, 128])
        vbf = sb("vbf", [128, 256], BF16)
        rsb = sb("rsb", [128, 256])
        t1 = sb("t1", [128, 256])
        sp_ = sb("sp", [128, 256])
        EP = sb("EP", [128, 2, 128])
        EN = sb("EN", [128, 2, 128])
        EE = sb("EE", [128, 128])
        decb = sb("decb", [128, 2])
        qd = sb("qd", [128, 2, 128], BF16)
        ki = sb("ki", [128, 2, 128], BF16)
        ke = sb("ke", [128, 128], BF16)
        Asb = sb("Asb", [128, 256])
        Bsb = sb("Bsb", [128, 256])
        PT = sb("PT", [128, 2, 128], BF16)
        S32 = [sb(f"S32_{i}", [128, 128]) for i in range(2)]
        Sbf = [sb(f"Sbf_{i}", [128, 128], BF16) for i in range(2)]
        GR = sb("GR", [128, 256])
        ssq = sb("ssq", [128, 2])
        rstd = sb("rstd", [128, 2])
        junk = sb("junk", [128, 128])
        oa = sb("oa", [128, 256], BF16)
        Ssb = [sb(f"Ssb{i}", [128, 5, 64]) for i in range(4)]
        Pbf = [sb(f"Pbf{i}", [128, 5, 64], BF16) for i in range(4)]
        rcp = [sb(f"rcp{i}", [128, 2]) for i in range(2)]
        onb = [sb(f"onb{i}", [128, 128], BF16) for i in range(2)]
        xr = [sb(f"xr{i}", [128, D]) for i in range(3)]
        bst = sb("bst", [128, 2, 6])
        mv = sb("mv", [128, 2])
        lnr = sb("lnr", [128, 2])
        nmr = sb("nmr", [128, 1])
        xhb = [sb(f"xhb{i}", [128, D], BF16) for i in range(2)]
        sg = [sb(f"sg{i}", [128, 512]) for i in range(2)]
        PS = [psb(f"ps{i}") for i in range(8)]
        PSb = [p[:].bitcast(BF16) for p in PS]
        BK = [Res(f"bank{i}", excl=True) for i in range(8)]

        P = Prog()

        mixT = arA[:, 0:16384].rearrange("p (k t) -> p k t", k=8)
        qbT = arA[:, 16384:18432]
        kbT = arA[:, 18432:20480]
        vbA = arA[:, 20480:22560].rearrange("p (t h e) -> p t h e", t=16, h=2)
        act = arA[:, 0:22528].rearrange("p (j t) -> p j t", j=NJ)
        woutT = arB[:, 0:8192].rearrange("p (k c) -> p k c", k=8)
        wGf = arB[:, 8192:10496].rearrange("p (k c) -> p k c", k=8)
        wGt = arB[:, 10496:15616].rearrange("p (k c) -> p k c", k=8)
        wN = arB[:, 15616:18688].rearrange("p (k c) -> p k c", k=8)
        SbB = arB[:, 18688:22784].rearrange("p (n v) -> p n v", n=32)
        w2T = arB[:, 0:22528].rearrange("p (j c) -> p j c", j=NJ)

        class _TSet:
            pass
        _tspec = [("ktm", 128, F32), ("vbf", 256, BF16), ("rsb", 256, F32), ("t1", 256, F32), ("sp_", 256, F32), ("EP", 256, F32),
                  ("EN", 256, F32), ("EE", 128, F32), ("decb", 2, F32), ("qd", 256, BF16), ("ki", 256, BF16), ("ke", 128, BF16),
                  ("Asb", 256, F32), ("Bsb", 256, F32), ("PT", 256, BF16), ("GR", 256, F32), ("ssq", 2, F32), ("rstd", 2, F32),
                  ("junk", 128, F32), ("oa", 256, BF16)]
        _t0 = {"ktm": ktm, "vbf": vbf, "rsb": rsb, "t1": t1, "sp_": sp_, "EP": EP, "EN": EN, "EE": EE, "decb": decb, "qd": qd, "ki": ki,
               "ke": ke, "Asb": Asb, "Bsb": Bsb, "PT": PT, "GR": GR, "ssq": ssq, "rstd": rstd, "junk": junk, "oa": oa}
        TB = [_TSet(), _TSet()]
        TB[0].i, TB[1].i = 0, 1
        _off = 0
        _tres = {}
        for (nm_, n_, dt_) in _tspec:
            setattr(TB[0], nm_, _t0[nm_])
            w_ = n_ * (2 if dt_ == F32 else 1)
            v_ = arB[:, _off:_off + w_]
            if dt_ == F32:
                v_ = v_.bitcast(F32)
            if nm_ in ("EP", "EN", "qd", "ki", "PT"):
                v_ = v_.rearrange("p (d t) -> p d t", d=2)
            setattr(TB[1], nm_, v_)
            _tres[nm_] = (_off, _off + w_)
            _off += w_ + (w_ % 2)
        assert _off <= 8192

        RES = {}
        arena_lists = {"A": [], "B": []}

        def R(name, key=0, arena=None, lo=0, hi=0):
            k = (name, key)
            r = RES.get(k)
            if r is None:
                r = Res(f"{name}{key}")
                RES[k] = r
                if arena is not None:
                    r.lo, r.hi = lo, hi
                    al = []
                    for o in arena_lists[arena]:
                        if o.lo < hi and lo < o.hi:
                            al.append(o)
                            o.alias = tuple(o.alias) + (r,)
                    r.alias = tuple(al)
                    arena_lists[arena].append(r)
            return r

        def Rmix(kc, t):
            return R("mixT", (kc, t), "A", kc * 2048 + t * 128, kc * 2048 + t * 128 + 128)

        def Rqb(t):
            return R("qbT", t, "A", 16384 + t * 128, 16384 + t * 128 + 128)

        def Rkb(t):
            return R("kbT", t, "A", 18432 + t * 128, 18432 + t * 128 + 128)

        def Rvb(t):
            return R("vbA", t, "A", 20480 + t * 130, 20480 + t * 130 + 130)

        def Ract(j, blk):
            return R("act", (j, blk), "A", j * 1024 + blk * 512, j * 1024 + blk * 512 + 512)

        Rwout = lambda: R("wout", 0, "B", 0, 8192)
        RwGf = lambda: R("wGf", 0, "B", 8192, 10496)
        RwGt = lambda: R("wGt", 0, "B", 10496, 15616)
        RwN = lambda: R("wN", 0, "B", 15616, 18688)
        RSb = lambda n: R("SbB", n, "B", 18688 + n * 128, 18688 + n * 128 + 128)
        Rw2 = lambda j: R("w2T", j, "B", j * 1024, j * 1024 + 1024)
        RXT = lambda t: R("XT", t)
        _rn = {"sp": "sp_"}
        TB[0].R = lambda name: R(name + "_s0")
        TB[1].R = lambda name: R(name + "_s1", 0, "B", *_tres[_rn.get(name, name)])
        for kc in range(8):
            for t in range(NT):
                Rmix(kc, t)
        for t in range(NT):
            Rqb(t), Rkb(t), Rvb(t)
        for j in range(NJ):
            Ract(j, 0), Ract(j, 1), Rw2(j)
        Rwout(), RwGf(), RwGt(), RwN()
        for n in range(32):
            RSb(n)

        DSF = (PS[6][:, 0:128], PS[5][:, 384:512])
        c_ident = cst[:, 0:128]
        c_triF = cst[:, 128:256]
        c_triB = cst[:, 256:384]
        c_triSF = cst[:, 384:512]
        c_triSB = cst[:, 512:640]
        c_mF = cst[:, 640:768]
        c_mB = cst[:, 768:896]
        c_cind = cst[:, 896:898]

        P.dma(lambda e: e.dma_start(out=cst[:], in_=cst_d[:, :]), writes=[R("cst")], dom="l_cst")
        P.dma(lambda e: e.dma_start(out=natM[:].rearrange("p j q -> p (j q)"), in_=natm_d[:, :]), writes=[R("natM")], dom="l_natM")
        P.op("dve", lambda e: e.tensor_copy(out=identb[:], in_=c_ident), reads=[R("cst")], writes=[R("identb")])
        P.op("pool", lambda e: e.memset(lrT[32:33, :], 1.0), writes=[R("lrT1")])

        def tap(name, src_ap, res, dst=None):
            if name in tap_d:
                d = tap_d[name] if dst is None else dst
                P.dma(lambda e: e.dma_start(out=d, in_=src_ap), reads=res, dom="s_tap")

        def phase_x0(s):
            for t in reversed(range(NT)):
                b = t % 2
                P.dma(lambda e, t=t, b=b: e.dma_start(out=xhb[b][:], in_=x_d[s, t * 128:(t + 1) * 128, :]),
                      writes=[R("xhb", b)], q="pool", dom=f"l_xhb{b}")
                transposes_to_XT(xhb[b], R("xhb", b), t, 6 + b, None)

        def transposes_to_XT(src_bf, src_res, t, bank, scale_bias):
            def tr(e):
                ins = None
                for kc in range(8):
                    ins = e.transpose(out=PSb[bank][:, kc * 128:(kc + 1) * 128], in_=src_bf[:, kc * 128:(kc + 1) * 128],
                                      identity=identb[:])
                return ins
            P.op("pe", tr, reads=[src_res, R("identb")], writes=[BK[bank]])
            dst = XT[:, :, t * 128:(t + 1) * 128]
            src = PSb[bank][:, 0:1024].rearrange("p (k c) -> p k c", k=8)
            P.op("dve", lambda e: e.tensor_copy(out=dst, in_=src), reads=[BK[bank]], writes=[RXT(t)])

        def load_layer_tables(l):
            P.dma(lambda e: e.dma_start(out=w2g[:], in_=w2g_d[l, :, :]), writes=[R("w2g")], dom="l_w2g")
            P.op("dve", lambda e: e.tensor_copy(out=w2gb[:], in_=w2g[:]), reads=[R("w2g")], writes=[R("w2gb")])
            P.dma(lambda e: e.dma_start(out=gnt[:], in_=gn_d[l, :, :]), writes=[R("gnt")], dom="l_gnt")

        def gla_pass(s, l, p):
            wl = w_in_d[l].rearrange("(k q) c -> q k c", q=128)
            for (dst, c0, n) in ((wGf[:, :, 0:128], C_QA + 128 * p, 128), (wGf[:, :, 128:256], C_KA + 128 * p, 128),
                                 (wGf[:, :, 256:288], C_LR, 32)):
                P.dma(lambda e, dst=dst, c0=c0, n=n: e.dma_start(out=dst, in_=wl[:, :, c0:c0 + n]),
                      writes=[RwGf()], q="pool", dom="l_wGf")
            for (dst, c0, n) in ((wGt[:, :, 0:128], C_KA + 128 * p, 128), (wGt[:, :, 128:384], C_VA + 256 * p, 256),
                                 (wGt[:, :, 384:640], C_RA + 256 * p, 256)):
                P.dma(lambda e, dst=dst, c0=c0, n=n: e.dma_start(out=dst, in_=wl[:, :, c0:c0 + n]),
                      writes=[RwGt()], q="pool", dom="l_wGt")
            gf0 = 128 * p
            gb0 = 256 + 128 * p

            def proj_fm(blk, c0, n, bank, m0=0):
                def f(e):
                    ins = None
                    for kc in range(8):
                        ins = e.matmul(PS[bank][m0:m0 + n, 0:512], lhsT=wGf[:, kc, c0:c0 + n],
                                       rhs=XT[:, kc, blk * 512:(blk + 1) * 512], start=(kc == 0), stop=(kc == 7))
                    return ins
                P.op("pe", f, reads=[RwGf()] + [RXT(4 * blk + i) for i in range(4)], writes=[BK[bank]])

            def proj_tm(t, c0, n, bank):
                def f(e):
                    ins = None
                    for kc in range(8):
                        ins = e.matmul(PS[bank][:, 0:n], lhsT=XT[:, kc, t * 128:(t + 1) * 128], rhs=wGt[:, kc, c0:c0 + n],
                                       start=(kc == 0), stop=(kc == 7))
                    return ins
                P.op("pe", f, reads=[RwGt(), RXT(t)], writes=[BK[bank]])

            def lr_block(blk):
                proj_fm(blk, 256, 32, 2)
                P.op("act", lambda e: e.copy(out=lrT[0:32, :], in_=PS[2][0:32, 0:512]), reads=[BK[2]], writes=[R("lrT")])

            def softplus_neg(ncols, B):
                P.op("act", lambda e: e.activation(out=B.t1[:, 0:ncols], in_=PS[2][:, 0:ncols], func=AF.Exp, scale=-1.0),
                     reads=[BK[2]], writes=[B.R("t1")])
                P.op("act", lambda e: e.activation(out=B.sp_[:, 0:ncols], in_=B.t1[:, 0:ncols], func=AF.Ln, bias=1.0),
                     reads=[B.R("t1")], writes=[B.R("sp")])

            def tile_a(t, ti, blk, cur, B):
                t = 4 * blk + ti
                proj_tm(t, 0, 384, 3)
                P.op("act", lambda e: e.copy(out=B.ktm[:], in_=PS[3][:, 0:128]), reads=[BK[3]], writes=[B.R("ktm")])
                P.op("dve", lambda e: e.tensor_copy(out=B.vbf[:], in_=PS[3][:, 128:384]), reads=[BK[3]], writes=[B.R("vbf")])
                ck("A2")
                P.op("pe", lambda e, ti=ti: e.matmul(PS[2][:, 0:128], lhsT=lrT[0:33, ti * 128:(ti + 1) * 128],
                                                     rhs=w2gb[0:33, gb0:gb0 + 128], start=True, stop=True),
                     reads=[R("lrT"), R("lrT1"), R("w2gb")], writes=[BK[2]])
                softplus_neg(128, B)
                ck("A3")

                def cums(e):
                    e.matmul(PS[5][:, 0:128], lhsT=c_triSB, rhs=B.sp_[:, 0:128], start=True, stop=True)
                    return e.matmul(PS[5][:, 128:130], lhsT=B.sp_[:, 0:128], rhs=c_cind, start=True, stop=True)
                P.op("pe", cums, reads=[B.R("sp"), R("cst")], writes=[BK[5]])
                P.op("act", lambda e: e.activation(out=B.EE[:], in_=PS[5][:, 0:128], func=AF.Exp), reads=[BK[5]], writes=[B.R("EE")])
                P.op("act", lambda e: e.activation(out=B.decb[:], in_=PS[5][:, 128:130], func=AF.Exp), reads=[BK[5]], writes=[B.R("decb")])
                P.op("dve", lambda e: e.tensor_tensor(out=B.ke[:], in0=B.ktm[:], in1=B.EE[:], op=ALU.mult),
                     reads=[B.R("ktm"), B.R("EE")], writes=[B.R("ke")])

                def dsb(e):
                    ins = None
                    for j in range(2):
                        for hh in range(2):
                            ins = e.matmul(PS[6 + j][64 * hh:64 * hh + 64, 0:128],
                                           lhsT=B.ke[64 * j:64 * j + 64, 64 * hh:64 * hh + 64],
                                           rhs=B.vbf[64 * j:64 * j + 64, 128 * hh:128 * hh + 128],
                                           start=True, stop=True, tile_position=(64 * j, 64 * hh))
                    return ins
                ck("A4")
                P.op("pe", dsb, reads=[B.R("ke"), B.R("vbf")], writes=[BK[6], BK[7]])
                ck("A5")
                for j in (1, 0):
                    n = 2 * t + j
                    P.op("pool", lambda e, n=n, cur=cur: e.tensor_copy(out=SbB[:, n, :], in_=S32[cur][:]),
                         reads=[R("S32", cur)], writes=[RSb(n)])
                    P.op("dve", lambda e, j=j, cur=cur: e.scalar_tensor_tensor(
                        out=S32[1 - cur][:], in0=S32[cur][:], scalar=B.decb[:, j:j + 1], in1=PS[6 + j][:, 0:128],
                        op0=ALU.mult, op1=ALU.add),
                        reads=[R("S32", cur), B.R("decb"), BK[6 + j]], writes=[R("S32", 1 - cur)])
                    cur = 1 - cur
                return cur

            def tile_b(t, ti, blk, cur, B):
                t = 4 * blk + ti
                tc0 = ti * 128
                proj_tm(t, 0, 384, 3)
                proj_tm(t, 384, 256, 4)
                P.op("act", lambda e: e.copy(out=B.ktm[:], in_=PS[3][:, 0:128]), reads=[BK[3]], writes=[B.R("ktm")])
                P.op("dve", lambda e: e.tensor_copy(out=B.vbf[:], in_=PS[3][:, 128:384]), reads=[BK[3]], writes=[B.R("vbf")])
                P.op("act", lambda e: e.activation(out=B.rsb[:], in_=PS[4][:, 0:256], func=AF.Exp, scale=-1.0), reads=[BK[4]], writes=[B.R("rsb")])
                P.op("act", lambda e: e.activation(out=B.rsb[:], in_=B.rsb[:], func=AF.Ln, bias=1.0), reads=[B.R("rsb")], writes=[B.R("rsb")])
                P.op("act", lambda e: e.activation(out=B.rsb[:], in_=B.rsb[:], func=AF.Exp, scale=-1.0), reads=[B.R("rsb")], writes=[B.R("rsb")])
                P.op("pool", lambda e: e.tensor_tensor(out=B.rsb[:], in0=B.rsb[:], in1=gnt[:, 256 * p:256 * p + 256], op=ALU.mult),
                     reads=[B.R("rsb"), R("gnt")], writes=[B.R("rsb")])
                P.op("dve", lambda e: e.tensor_tensor(out=B.GR[:], in0=PS[4][:, 0:256], in1=B.rsb[:], op=ALU.mult),
                     reads=[BK[4], B.R("rsb")], writes=[B.R("GR")])

                def zmm(e, tc0=tc0):
                    e.matmul(PS[2][:, 0:128], lhsT=lrT[0:33, tc0:tc0 + 128], rhs=w2gb[0:33, gf0:gf0 + 128], start=True, stop=True)
                    return e.matmul(PS[2][:, 128:256], lhsT=lrT[0:33, tc0:tc0 + 128], rhs=w2gb[0:33, gb0:gb0 + 128], start=True, stop=True)
                P.op("pe", zmm, reads=[R("lrT"), R("lrT1"), R("w2gb")], writes=[BK[2]])
                softplus_neg(256, B)

                def cums2(e):
                    e.matmul(PS[5][:, 0:128], lhsT=B.sp_[:, 0:128], rhs=c_triF, start=True, stop=True)
                    e.matmul(PS[5][:, 128:256], lhsT=B.sp_[:, 128:256], rhs=c_triB, start=True, stop=True)
                    return e.matmul(PS[5][:, 256:384], lhsT=c_triSF, rhs=B.sp_[:, 0:128], start=True, stop=True)
                P.op("pe", cums2, reads=[B.R("sp"), R("cst")], writes=[BK[5]])
                bc = PS[5][:, 0:256].rearrange("p (d t) -> p d t", d=2)
                P.op("act", lambda e, bc=bc: e.activation(out=B.EP[:], in_=bc, func=AF.Exp), reads=[BK[5]], writes=[B.R("EP")])
                P.op("act", lambda e, bc=bc: e.activation(out=B.EN[:], in_=bc, func=AF.Exp, scale=-1.0), reads=[BK[5]], writes=[B.R("EN")])
                P.op("act", lambda e: e.activation(out=B.EE[:], in_=PS[5][:, 256:384], func=AF.Exp), reads=[BK[5]], writes=[B.R("EE")])
                for d in range(2):
                    P.op("dve", lambda e, d=d, tc0=tc0: e.scalar_tensor_tensor(
                        out=B.qd[:, d, :], in0=qraw[:, tc0:tc0 + 128], scalar=0.125, in1=B.EP[:, d, :], op0=ALU.mult, op1=ALU.mult),
                        reads=[R("qraw"), B.R("EP")], writes=[B.R("qd")])
                    P.op("dve", lambda e, d=d, tc0=tc0: e.tensor_tensor(out=B.ki[:, d, :], in0=kraw[:, tc0:tc0 + 128], in1=B.EN[:, d, :], op=ALU.mult),
                         reads=[R("kraw"), B.R("EN")], writes=[B.R("ki")])
                P.op("dve", lambda e: e.tensor_tensor(out=B.ke[:], in0=B.ktm[:], in1=B.EE[:], op=ALU.mult),
                     reads=[B.R("ktm"), B.R("EE")], writes=[B.R("ke")])

                def scores(e):
                    ins = None
                    for d in range(2):
                        for hh in range(2):
                            ins = e.matmul(PS[hh][:, d * 128:(d + 1) * 128],
                                           lhsT=B.ki[64 * hh:64 * hh + 64, d, :], rhs=B.qd[64 * hh:64 * hh + 64, d, :],
                                           start=True, stop=True, tile_position=(64 * hh, 0))
                    return ins
                P.op("pe", scores, reads=[B.R("ki"), B.R("qd")], writes=[BK[0], BK[1]])
                for hh in range(2):
                    P.op("dve", lambda e, hh=hh: e.tensor_tensor(out=B.Asb[:, hh * 128:(hh + 1) * 128], in0=PS[hh][:, 0:128],
                                                              in1=c_mF, op=ALU.mult), reads=[BK[hh], R("cst")], writes=[B.R("Asb")])
                    P.op("dve", lambda e, hh=hh: e.tensor_tensor(out=B.Bsb[:, hh * 128:(hh + 1) * 128], in0=PS[hh][:, 128:256],
                                                              in1=c_mB, op=ALU.mult), reads=[BK[hh], R("cst")], writes=[B.R("Bsb")])
                P.op("pool", lambda e: e.tensor_tensor(out=B.PT[:].rearrange("p h c -> p (h c)"), in0=B.Asb[:], in1=B.Bsb[:], op=ALU.add),
                     reads=[B.R("Asb"), B.R("Bsb")], writes=[B.R("PT")])

                def dsf(e):
                    ins = None
                    for j in range(2):
                        for hh in range(2):
                            ins = e.matmul(DSF[j][64 * hh:64 * hh + 64, :],
                                           lhsT=B.ke[64 * j:64 * j + 64, 64 * hh:64 * hh + 64],
                                           rhs=B.vbf[64 * j:64 * j + 64, 128 * hh:128 * hh + 128],
                                           start=True, stop=True, tile_position=(64 * j, 64 * hh))
                    return ins
                P.op("pe", dsf, reads=[B.R("ke"), B.R("vbf")], writes=[BK[6], BK[5]])
                c0 = cur
                for j in range(2):
                    P.op("dve", lambda e, j=j, cur=cur: e.scalar_tensor_tensor(
                        out=S32[1 - cur][:], in0=S32[cur][:], scalar=B.EP[:, 0, 64 * j + 63:64 * j + 64],
                        in1=DSF[j][:, :], op0=ALU.mult, op1=ALU.add),
                        reads=[R("S32", cur), B.R("EP"), BK[(6, 5)[j]]], writes=[R("S32", 1 - cur)])
                    cur = 1 - cur
                    if j == 0:
                        P.op("pool", lambda e, cur=cur: e.tensor_copy(out=Sbf[1][:], in_=S32[cur][:]), reads=[R("S32", cur)], writes=[R("Sbf", 1)])
                def omm(e, t=t):
                    ins = None
                    for hh in range(2):
                        o_ = PS[7][:, hh * 128:(hh + 1) * 128]
                        e.matmul(o_, lhsT=B.PT[:, hh, :], rhs=B.vbf[:, 128 * hh:128 * hh + 128], start=True, stop=False)
                        for j in range(2):
                            n = 2 * t + j
                            oj = PS[7][64 * j:64 * j + 64, hh * 128:(hh + 1) * 128]
                            e.matmul(oj, lhsT=B.qd[64 * hh:64 * hh + 64, 0, 64 * j:64 * j + 64], rhs=Sbf[j][64 * hh:64 * hh + 64, :],
                                     start=False, stop=False, tile_position=(64 * hh, 64 * j))
                            ins = e.matmul(oj, lhsT=B.qd[64 * hh:64 * hh + 64, 1, 64 * j:64 * j + 64], rhs=SbB[64 * hh:64 * hh + 64, n, :],
                                           start=False, stop=True, tile_position=(64 * hh, 64 * j))
                    return ins
                P.op("pe", omm, reads=[B.R("PT"), B.R("vbf"), B.R("qd"), R("Sbf", 0), R("Sbf", 1), RSb(2 * t), RSb(2 * t + 1)], writes=[BK[7]])
                P.op("pool", lambda e, cur=cur: e.tensor_copy(out=Sbf[0][:], in_=S32[cur][:]), reads=[R("S32", cur)], writes=[R("Sbf", 0)])
                for hh in range(2):
                    P.op("act", lambda e, hh=hh: e.activation(out=B.junk[:], in_=PS[7][:, hh * 128:(hh + 1) * 128], func=AF.Square,
                                                             accum_out=B.ssq[:, hh:hh + 1]), reads=[BK[7]], writes=[B.R("ssq"), B.R("junk")])
                P.op("act", lambda e: e.activation(out=B.rstd[:], in_=B.ssq[:], func=AF.Ln, scale=1.0 / 128.0, bias=EPS), reads=[B.R("ssq")], writes=[B.R("rstd")])
                P.op("act", lambda e: e.activation(out=B.rstd[:], in_=B.rstd[:], func=AF.Exp, scale=-0.5), reads=[B.R("rstd")], writes=[B.R("rstd")])
                for hh in range(2):
                    P.op("dve", lambda e, hh=hh: e.scalar_tensor_tensor(
                        out=B.oa[:, hh * 128:(hh + 1) * 128], in0=PS[7][:, hh * 128:(hh + 1) * 128], scalar=B.rstd[:, hh:hh + 1],
                        in1=B.GR[:, hh * 128:(hh + 1) * 128], op0=ALU.mult, op1=ALU.mult),
                        reads=[BK[7], B.R("rstd"), B.R("GR")], writes=[B.R("oa")])

                def otr(e):
                    e.transpose(out=PSb[7][:, 512:640], in_=B.oa[:, 0:128], identity=identb[:])
                    return e.transpose(out=PSb[7][:, 640:768], in_=B.oa[:, 128:256], identity=identb[:])
                P.op("pe", otr, reads=[B.R("oa"), R("identb")], writes=[BK[7]])
                P.op("dve", lambda e, t=t: e.tensor_copy(out=mixT[:, 2 * p:2 * p + 2, t * 128:(t + 1) * 128],
                                                       in_=PSb[7][:, 512:768].rearrange("p (k c) -> p k c", k=2)),
                     reads=[BK[7]], writes=[Rmix(2 * p, t), Rmix(2 * p + 1, t)])
                return cur

            P.op("pool", lambda e: e.memset(S32[0][:], 0.0), writes=[R("S32", 0)])
            cur = 0
            gs = taps.get("_gstop")

            def ck(name):
                if gs == name:
                    raise _Stop()
            for blk in (3, 2, 1, 0):
                ck("A0")
                lr_block(blk)
                ck("A1")
                for ti in (3, 2, 1, 0):
                    cur = tile_a(4 * blk + ti, ti, blk, cur, TB[ti % 2])
            if taps.get("_gstop") == "A":
                return
            P.op("pool", lambda e: e.memset(S32[0][:], 0.0), writes=[R("S32", 0)])
            P.op("pool", lambda e: e.memset(Sbf[0][:], 0.0), writes=[R("Sbf", 0)])
            cur = 0
            for blk in range(4):
                proj_fm(blk, 0, 128, 0)
                proj_fm(blk, 128, 128, 1)
                lr_block(blk)
                P.op("act", lambda e: e.copy(out=qraw[:], in_=PS[0][:, :]), reads=[BK[0]], writes=[R("qraw")])
                P.op("dve", lambda e: e.tensor_copy(out=kraw[:], in_=PS[1][:, :]), reads=[BK[1]], writes=[R("kraw")])
                for ti in range(4):
                    cur = tile_b(4 * blk + ti, ti, blk, cur, TB[ti % 2])

        def nat_pass(s, l, c):
            wl = w_in_d[l].rearrange("(k q) c -> q k c", q=128)
            for (dst, c0) in ((wN[:, :, 0:128], C_QB + 128 * c), (wN[:, :, 128:256], C_KB + 128 * c), (wN[:, :, 256:384], C_VB + 128 * c)):
                P.dma(lambda e, dst=dst, c0=c0: e.dma_start(out=dst, in_=wl[:, :, c0:c0 + 128]), writes=[RwN()], q="pool", dom="l_wN")
            P.dma(lambda e: e.dma_start(out=natT[:].rearrange("p h j q -> p (h j q)"), in_=natb_d[l, c, :, :]), writes=[R("natT")], dom="l_natT")
            for hh in range(2):
                P.op("pool", lambda e, hh=hh: e.tensor_tensor(out=natT[:, hh, :, :], in0=natT[:, hh, :, :], in1=natM[:], op=ALU.add),
                     reads=[R("natT"), R("natM")], writes=[R("natT")])
            for i in range(2):
                P.op("pool", lambda e, i=i: e.memset(vbA[:, :, i, 64:65], 1.0), writes=[Rvb(t) for t in range(NT)])
            for blk in range(4):
                for (c0, bank) in ((0, 0), (128, 1)):
                    def f(e, c0=c0, bank=bank, blk=blk):
                        ins = None
                        for kc in range(8):
                            ins = e.matmul(PS[bank][:, 0:512], lhsT=wN[:, kc, c0:c0 + 128], rhs=XT[:, kc, blk * 512:(blk + 1) * 512],
                                           start=(kc == 0), stop=(kc == 7))
                        return ins
                    P.op("pe", f, reads=[RwN()] + [RXT(4 * blk + i) for i in range(4)], writes=[BK[bank]])
                P.op("act", lambda e, blk=blk: e.mul(out=qbT[:, blk * 512:(blk + 1) * 512], in_=PS[0][:, :], mul=0.125),
                     reads=[BK[0]], writes=[Rqb(4 * blk + i) for i in range(4)])
                P.op("dve", lambda e, blk=blk: e.tensor_copy(out=kbT[:, blk * 512:(blk + 1) * 512], in_=PS[1][:, :]),
                     reads=[BK[1]], writes=[Rkb(4 * blk + i) for i in range(4)])
                for ti in range(4):
                    t = 4 * blk + ti
                    vbk = (2, 7, 3, 4)[ti]

                    def fv(e, t=t, vbk=vbk):
                        ins = None
                        for kc in range(8):
                            ins = e.matmul(PS[vbk][:, 0:128], lhsT=XT[:, kc, t * 128:(t + 1) * 128], rhs=wN[:, kc, 256:384],
                                           start=(kc == 0), stop=(kc == 7))
                        return ins
                    P.op("pe", fv, reads=[RwN(), RXT(t)], writes=[BK[vbk]])
                    P.op("act" if ti % 2 == 0 else "dve", (lambda e, t=t, vbk=vbk: e.copy(out=vbA[:, t, :, 0:64], in_=PS[vbk][:, 0:128].rearrange("p (h d) -> p h d", h=2)))
                         if ti % 2 == 0 else (lambda e, t=t, vbk=vbk: e.tensor_copy(out=vbA[:, t, :, 0:64], in_=PS[vbk][:, 0:128].rearrange("p (h d) -> p h d", h=2))),
                         reads=[BK[vbk]], writes=[Rvb(t)])
            it = 0
            for t in range(NT):
                for rr in range(2):
                    r = 2 * t + rr
                    rs = min(max(r - 4, 0), 24)
                    if rs % 2 == 0:
                        a0, nch = rs // 2, 4
                        jb = [(rs - r + 7) + 2 * cc for cc in range(4)]
                        tab = lambda hh, jb=jb: natT[:, hh, jb[0]:jb[0] + 7:2, :]
                    else:
                        a0, nch = (rs - 1) // 2, 5
                        tab = lambda hh: natT[:, hh, 14:19, :]
                    for hh in range(2):
                        bank = 3 + (it % 4)
                        sbi = it % 4
                        it += 1

                        def smm(e, a0=a0, nch=nch, hh=hh, bank=bank, r=r):
                            ins = None
                            for cc in range(nch):
                                a = a0 + cc
                                ins = e.matmul(PS[bank][:, cc * 64:(cc + 1) * 64], lhsT=kbT[64 * hh:64 * hh + 64, a * 128:(a + 1) * 128],
                                               rhs=qbT[64 * hh:64 * hh + 64, r * 64:(r + 1) * 64], start=True, stop=True)
                            return ins
                        P.op("pe", smm, reads=[Rkb(a0 + cc) for cc in range(nch)] + [Rqb(t)], writes=[BK[bank]])
                        psv = PS[bank][:, 0:nch * 64].rearrange("p (c q) -> p c q", c=nch)
                        P.op("dve", lambda e, psv=psv, nch=nch, sbi=sbi, tab=tab, hh=hh: e.tensor_tensor(
                            out=Ssb[sbi][:, 0:nch, :], in0=psv, in1=tab(hh), op=ALU.add),
                            reads=[BK[bank], R("natT")], writes=[R("Ssb", sbi)])
                        P.op("act", lambda e, nch=nch, sbi=sbi: e.activation(out=Pbf[sbi][:, 0:nch, :], in_=Ssb[sbi][:, 0:nch, :], func=AF.Exp),
                             reads=[R("Ssb", sbi)], writes=[R("Pbf", sbi)])

                        def pv(e, a0=a0, nch=nch, hh=hh, sbi=sbi, rr=rr):
                            ins = None
                            for cc in range(nch):
                                a = a0 + cc
                                ins = e.matmul(PS[7][64 * rr:64 * rr + 64, hh * 65:hh * 65 + 65], lhsT=Pbf[sbi][:, cc, :],
                                               rhs=vbA[:, a, hh, :], start=(cc == 0), stop=(cc == nch - 1), tile_position=(0, 64 * rr))
                            return ins
                        P.op("pe", pv, reads=[R("Pbf", sbi)] + [Rvb(a0 + cc) for cc in range(nch)], writes=[BK[7]])
                ov = PS[7][:, 0:130].rearrange("p (h e) -> p h e", h=2)
                tb_ = t % 2
                P.op("dve", lambda e, ov=ov, tb_=tb_: e.reciprocal(out=rcp[tb_][:].rearrange("p (h o) -> p h o", o=1), in_=ov[:, :, 64:65]),
                     reads=[BK[7]], writes=[R("rcp", tb_)])
                for hh in range(2):
                    P.op("dve", lambda e, hh=hh, tb_=tb_: e.tensor_scalar(out=onb[tb_][:, hh * 64:(hh + 1) * 64], in0=PS[7][:, hh * 65:hh * 65 + 64],
                                                               scalar1=rcp[tb_][:, hh:hh + 1], scalar2=None, op0=ALU.mult),
                         reads=[BK[7], R("rcp", tb_)], writes=[R("onb", tb_)])
                P.op("pe", lambda e, tb_=tb_: e.transpose(out=PSb[2][:, 512:640], in_=onb[tb_][:], identity=identb[:]), reads=[R("onb", tb_), R("identb")], writes=[BK[2]])
                P.op("act", lambda e, t=t: e.copy(out=mixT[:, 4 + c, t * 128:(t + 1) * 128], in_=PSb[2][:, 512:640]),
                     reads=[BK[2]], writes=[Rmix(4 + c, t)])

        def layer_norm_tile(buf, bres, gi):
            xb = xr[buf]

            def st(e):
                e.bn_stats(out=bst[:, 0, :], in_=xb[:, 0:512])
                return e.bn_stats(out=bst[:, 1, :], in_=xb[:, 512:1024])
            P.op("dve", st, reads=[bres], writes=[R("bst")])
            P.op("dve", lambda e: e.bn_aggr(out=mv[:], in_=bst[:]), reads=[R("bst")], writes=[R("mv")])
            P.op("act", lambda e: e.activation(out=lnr[:, 0:1], in_=mv[:, 1:2], func=AF.Ln, bias=EPS), reads=[R("mv")], writes=[R("lnr")])
            P.op("act", lambda e: e.activation(out=lnr[:, 1:2], in_=lnr[:, 0:1], func=AF.Exp, scale=-0.5), reads=[R("lnr")], writes=[R("lnr")])
            P.op("dve", lambda e: e.scalar_tensor_tensor(out=xb[:], in0=xb[:], scalar=mv[:, 0:1], in1=lnt[:, 0, :], op0=ALU.subtract, op1=ALU.mult),
                 reads=[bres, R("mv"), R("lnt")], writes=[bres])
            P.op("act", lambda e: e.activation(out=xb[:], in_=xb[:], func=AF.Copy, scale=lnr[:, 1:2]), reads=[bres, R("lnr")], writes=[bres])
            P.op("pool", lambda e: e.tensor_tensor(out=xb[:], in0=xb[:], in1=lnt[:, 1, :], op=ALU.add), reads=[bres, R("lnt")], writes=[bres])

        def phase_o(s, l):
            wl = w_out_d[l].rearrange("(k q) c -> q k c", q=128)
            for h in range(2):
                P.dma(lambda e, h=h: e.dma_start(out=woutT[:, :, h * 512:(h + 1) * 512], in_=wl[:, :, h * 512:(h + 1) * 512]),
                      writes=[Rwout()], q="pool", dom="l_wout")
            P.dma(lambda e: e.dma_start(out=lnt[:], in_=lnp_d[l, 0:2].rearrange("g p d -> p g d")), writes=[R("lnt")], dom="l_lnt")
            src = x_d[s] if l == 0 else xs_d
            for t in range(NT):
                b = t % 3
                bres = R("xr", b)
                P.dma(lambda e, t=t, b=b: e.dma_start(out=xr[b][:], in_=src[t * 128:(t + 1) * 128, :]),
                      reads=([R("xs", t)] if l > 0 else []), writes=[bres], dom=f"l_xr{b}")
                bk = 2 * (t % 2)
                for h in range(2):
                    def f(e, h=h, t=t, bk=bk):
                        ins = None
                        for kc in range(8):
                            ins = e.matmul(PS[bk + h][:, :], lhsT=mixT[:, kc, t * 128:(t + 1) * 128], rhs=woutT[:, kc, h * 512:(h + 1) * 512],
                                           start=(kc == 0), stop=(kc == 7))
                        return ins
                    P.op("pe", f, reads=[Rwout()] + [Rmix(kc, t) for kc in range(8)], writes=[BK[bk + h]])
                    P.op("dve", lambda e, h=h, b=b, bk=bk: e.scalar_tensor_tensor(
                        out=xr[b][:, h * 512:(h + 1) * 512], in0=xr[b][:, h * 512:(h + 1) * 512], scalar=ALPHA, in1=PS[bk + h][:, :],
                        op0=ALU.mult, op1=ALU.add), reads=[bres, BK[bk + h]], writes=[bres])
                layer_norm_tile(b, bres, 0)
                P.dma(lambda e, t=t, b=b: e.dma_start(out=x1s_d[t * 128:(t + 1) * 128, :], in_=xr[b][:]), reads=[bres], writes=[R("x1s", t)], dom=f"s_xr{b}")
                hb = t % 2
                P.op("act", lambda e, b=b, hb=hb: e.copy(out=xhb[hb][:], in_=xr[b][:]), reads=[bres], writes=[R("xhb", hb)])
                transposes_to_XT(xhb[hb], R("xhb", hb), t, 4 + hb, None)

        def phase_f(s, l):
            w1l = w1_d[l].rearrange("(k q) c -> q k c", q=128)
            w2l = w2_d[l].rearrange("(j q) c -> q j c", q=128)
            for j0 in range(0, NJ, 6):
                j1 = min(NJ, j0 + 6)
                P.dma(lambda e, j0=j0, j1=j1: e.dma_start(out=w2T[:, j0:j1, :], in_=w2l[:, j0:j1, :]),
                      writes=[Rw2(j) for j in range(j0, j1)], q="pool", dom=f"l_w2_{j0}")
            P.dma(lambda e: e.dma_start(out=lnt[:], in_=lnp_d[l, 2:4].rearrange("g p d -> p g d")), writes=[R("lnt")], dom="l_lnt")
            it = 0
            for half in range(2):
                for j in range(NJ):
                    wb = j % 2
                    P.dma(lambda e, j=j, wb=wb: e.dma_start(out=w1b[wb][:, :, 0:128], in_=w1l[:, :, j * 128:(j + 1) * 128]),
                          writes=[R("w1b", wb)], q="pool", dom=f"l_w1b{wb}")
                    P.dma(lambda e, j=j, wb=wb: e.dma_start(out=w1b[wb][:, :, 128:256], in_=w1l[:, :, FH + j * 128:FH + (j + 1) * 128]),
                          writes=[R("w1b", wb)], q="pool", dom=f"l_w1b{wb}")
                    for bl in range(2):
                        tb = half * 2 + bl
                        gb, ub = (0, 1) if it % 2 == 0 else (2, 3)
                        sgi = it % 2
                        it += 1
                        for (bank, c0) in ((gb, 0), (ub, 128)):
                            def f(e, bank=bank, c0=c0, wb=wb, tb=tb):
                                ins = None
                                for kc in range(8):
                                    ins = e.matmul(PS[bank][:, :], lhsT=w1b[wb][:, kc, c0:c0 + 128], rhs=XT[:, kc, tb * 512:(tb + 1) * 512],
                                                   start=(kc == 0), stop=(kc == 7))
                                return ins
                            P.op("pe", f, reads=[R("w1b", wb)] + [RXT(4 * tb + i) for i in range(4)], writes=[BK[bank]])
                        P.op("act", lambda e, gb=gb, sgi=sgi: e.activation(out=sg[sgi][:], in_=PS[gb][:, :], func=AF.Silu),
                             reads=[BK[gb]], writes=[R("sg", sgi)])
                        P.op("dve", lambda e, ub=ub, sgi=sgi, j=j, bl=bl: e.tensor_tensor(
                            out=act[:, j, bl * 512:(bl + 1) * 512], in0=PS[ub][:, :], in1=sg[sgi][:], op=ALU.mult),
                            reads=[BK[ub], R("sg", sgi)], writes=[Ract(j, bl)])
                for tt in range(8):
                    t = half * 8 + tt
                    b = t % 3
                    bres = R("xr", b)
                    P.dma(lambda e, t=t, b=b: e.dma_start(out=xr[b][:], in_=x1s_d[t * 128:(t + 1) * 128, :]),
                          reads=[R("x1s", t)], writes=[bres], dom=f"l_xr{b}")
                    bk = 4 + 2 * (t % 2)
                    for h in range(2):
                        def f(e, h=h, tt=tt, bk=bk):
                            ins = None
                            for j in range(NJ):
                                ins = e.matmul(PS[bk + h][:, :], lhsT=act[:, j, tt * 128:(tt + 1) * 128], rhs=w2T[:, j, h * 512:(h + 1) * 512],
                                               start=(j == 0), stop=(j == NJ - 1))
                            return ins
                        P.op("pe", f, reads=[Rw2(j) for j in range(NJ)] + [Ract(j, tt // 4) for j in range(NJ)], writes=[BK[bk + h]])
                        P.op("dve", lambda e, h=h, b=b, bk=bk: e.scalar_tensor_tensor(
                            out=xr[b][:, h * 512:(h + 1) * 512], in0=xr[b][:, h * 512:(h + 1) * 512], scalar=ALPHA, in1=PS[bk + h][:, :],
                            op0=ALU.mult, op1=ALU.add), reads=[bres, BK[bk + h]], writes=[bres])
                    layer_norm_tile(b, bres, 2)
                    if l == nl - 1:
                        P.dma(lambda e, t=t, b=b: e.dma_start(out=y_d[s, t * 128:(t + 1) * 128, :], in_=xr[b][:]), reads=[bres], dom=f"s_xr{b}")
                    else:
                        P.dma(lambda e, t=t, b=b: e.dma_start(out=xs_d[t * 128:(t + 1) * 128, :], in_=xr[b][:]), reads=[bres],
                              writes=[R("xs", t)], dom=f"s_xr{b}")
                        hb = t % 2
                        P.op("act", lambda e, b=b, hb=hb: e.copy(out=xhb[hb][:], in_=xr[b][:]), reads=[bres], writes=[R("xhb", hb)])
                        transposes_to_XT(xhb[hb], R("xhb", hb), t, hb, None)

        phases = taps.get("_phases", ("x0", "gla", "nat", "o", "f"))
        for s in range(ns):
            if "x0" in phases:
                phase_x0(s)
            for l in range(nl):
                load_layer_tables(l)
                if "gla" in phases:
                    try:
                        for p in range(2):
                            gla_pass(s, l, p)
                    except _Stop:
                        pass
                if "nat" in phases:
                    for c in range(4):
                        nat_pass(s, l, c)
                if "mix" in tap_d and s == 0 and l == 0:
                    for kc in ([0, 1, 2, 3] if "gla" in phases else []) + ([4, 5, 6, 7] if "nat" in phases else []):
                        P.op("dve", lambda e, kc=kc: e.tensor_copy(out=xr[0][:, 0:1024], in_=mixT[:, kc, 0:1024]),
                             reads=[Rmix(kc, t) for t in range(8)], writes=[R("xr", 0)])
                        P.dma(lambda e, kc=kc: e.dma_start(out=tap_d["mix"][kc, :, 0:1024], in_=xr[0][:, 0:1024]), reads=[R("xr", 0)], dom="s_xr0")
                        P.op("dve", lambda e, kc=kc: e.tensor_copy(out=xr[0][:, 0:1024], in_=mixT[:, kc, 1024:2048]),
                             reads=[Rmix(kc, t) for t in range(8, 16)], writes=[R("xr", 0)])
                        P.dma(lambda e, kc=kc: e.dma_start(out=tap_d["mix"][kc, :, 1024:2048], in_=xr[0][:, 0:1024]), reads=[R("xr", 0)], dom="s_xr0")
                if "o" in phases:
                    phase_o(s, l)
                if "f" in phases:
                    phase_f(s, l)
        doms = ["pe", "act", "dve", "pool"]
        for o in P.ops:
            if o.dom not in doms:
                doms.append(o.dom)
        sems = {d: es.enter_context(nc.semaphore(d)) for d in doms}
        block = es.enter_context(nc.Block())
        P.emit(block, sems)
        nc._prog_stats = (len(P.ops), len(doms), getattr(P, 'est_ns', None))
    return nc


def _prep_shared(w_in, gla_gate_w2, gla_gate_b, gla_norm_g, nat_rpb, w_out, ln1_g, ln1_b, w_ffn_in, w_ffn_out, ln2_g, ln2_b):
    f = lambda a: np.ascontiguousarray(np.asarray(a, dtype=np.float32))
    w2g = np.zeros((NL, 33, 512), np.float32)
    w2g[:, 0:16, 0:256] = gla_gate_w2[:, 0]
    w2g[:, 16:32, 256:512] = gla_gate_w2[:, 1]
    w2g[:, 32, 0:256] = gla_gate_b[:, 0]
    w2g[:, 32, 256:512] = gla_gate_b[:, 1]
    gn = np.ascontiguousarray(np.broadcast_to(np.asarray(gla_norm_g, np.float32)[:, None, :], (NL, 128, 512)))
    lnp = np.stack([ln1_g, ln1_b, ln2_g, ln2_b], axis=1).astype(np.float32)
    lnp = np.ascontiguousarray(np.broadcast_to(lnp[:, :, None, :], (NL, 4, 128, D)))
    ro, co, valid = _nat_index_tables()
    rpb = np.asarray(nat_rpb, np.float32)
    g = rpb[:, :, ro, co]
    g = g.reshape(NL, 4, 2, 128, 19, 64).transpose(0, 1, 3, 2, 4, 5)
    natb = np.ascontiguousarray(g.reshape(NL, 4, 128, 2 * 19 * 64))
    natm = np.where(valid, np.float32(0.0), np.float32(MASKV)).astype(np.float32).reshape(128, 19 * 64)
    return {
        "w_in": f(w_in), "w_out": f(w_out), "w1": f(w_ffn_in), "w2": f(w_ffn_out),
        "w2g": w2g, "gn": gn, "lnp": lnp, "natb": natb, "natm": np.ascontiguousarray(natm), "cst": _host_consts(),
    }


_NC_CACHE = {}


def kernel(x_prompt, x_sample, w_in, gla_gate_w2, gla_gate_b, gla_norm_g, nat_rpb, w_out,
           ln1_g, ln1_b, w_ffn_in, w_ffn_out, ln2_g, ln2_b):
    xp = np.asarray(x_prompt, np.float32)
    xsm = np.asarray(x_sample, np.float32)
    shared = _prep_shared(np.asarray(w_in), np.asarray(gla_gate_w2), np.asarray(gla_gate_b), np.asarray(gla_norm_g),
                          np.asarray(nat_rpb), np.asarray(w_out), np.asarray(ln1_g), np.asarray(ln1_b),
                          np.asarray(w_ffn_in), np.asarray(w_ffn_out), np.asarray(ln2_g), np.asarray(ln2_b))
    in_maps = []
    for i in range(NCORES):
        xi = np.ascontiguousarray(np.concatenate([xp[4 * i:4 * i + 4], xsm[2 * i:2 * i + 2]], axis=0))
        m = {"x": xi}
        m.update(shared)
        in_maps.append(m)
    if "nc" not in _NC_CACHE:
        _NC_CACHE["nc"] = build()
    res = run_bass_kernel_spmd(_NC_CACHE["nc"], in_maps, core_ids=list(range(NCORES)))
    ys = [np.asarray(r["y"]) for r in res.results]
    y_prompt = np.concatenate([y[0:4] for y in ys], axis=0).astype(np.float32)
    y_sample = np.concatenate([y[4:6] for y in ys], axis=0).astype(np.float32)
    return (y_prompt, y_sample)
```

```python
import numpy as np
from contextlib import ExitStack
import concourse.bass as bass
import concourse.mybir as mybir
from concourse.bass_utils import run_bass_kernel_spmd

F32 = mybir.dt.float32
BF16 = mybir.dt.bfloat16
AF = mybir.ActivationFunctionType
ALU = mybir.AluOpType

COMPUTE = ("pe", "act", "dve", "pool")
import os as _os
_MODEL_NOWAR = bool(_os.environ.get("MODEL_NOWAR"))
_MODEL_DROP = tuple(x for x in _os.environ.get("MODEL_DROP", "").split(",") if x)
_MODEL_KEEP = tuple(_os.environ.get("MODEL_KEEP", "XT,xs,x1s").split(","))


class Res:
    __slots__ = ("name", "w", "r", "excl", "alias", "lo", "hi")

    def __init__(self, name, excl=False):
        self.name = name
        self.w = None
        self.r = {}
        self.excl = excl
        self.alias = ()
        self.lo = 0
        self.hi = 0


class Op:
    __slots__ = ("eng", "dom", "fn", "deps", "odeps", "signal", "count", "is_dma", "cost", "lat", "idx", "done", "nsucc", "pos")

    def __init__(self, eng, dom, fn, is_dma, cost, lat):
        self.eng = eng
        self.dom = dom
        self.fn = fn
        self.deps = ()
        self.odeps = ()
        self.signal = False
        self.count = 0
        self.is_dma = is_dma
        self.cost = cost
        self.lat = lat
        self.idx = 0
        self.done = -1.0


class _Stop(Exception):
    pass


def _fsize(ap):
    n = 1
    for d in ap.shape[1:]:
        n *= int(d)
    return n


class _AttachEng:
    def __init__(self, eng, sem, val):
        self._e = eng
        self._sem = sem
        self._val = val
        self._done = False

    def __getattr__(self, name):
        f = getattr(self._e, name)

        def g(*a, **kw):
            r = f(*a, **kw)
            if not self._done:
                r._wait_ge(self._sem, self._val)
                self._done = True
            return r
        return g


class _FakeIns:
    def then_inc(self, *a, **k):
        return self

    def _wait_ge(self, *a, **k):
        return self


class _FakeEng:
    def __init__(self, kind):
        self.kind = kind
        self.total = 0.0

    def matmul(self, out, lhsT=None, rhs=None, **kw):
        n = _fsize(rhs)
        f = 4.0 if rhs.dtype == F32 else 1.0
        self.total += f * max(n, 64) / 1.95 + 10.0
        return _FakeIns()

    def transpose(self, out=None, in_=None, identity=None, **kw):
        self.total += 70.0
        return _FakeIns()

    def dma_start(self, out=None, in_=None, **kw):
        n = 1
        for d in in_.shape:
            n *= int(d)
        self.total += n * 4 / 250.0
        return _FakeIns()

    def __getattr__(self, name):
        def f(*a, **kw):
            ap = kw.get("out", None)
            if ap is None:
                ap = kw.get("ap", a[0] if a else None)
            n = _fsize(ap) if ap is not None else 64
            if self.kind == "act":
                self.total += 230.0 + n / 1.1
            elif self.kind == "dve":
                self.total += 110.0 + n / 0.9
            else:
                self.total += 350.0 + n / 0.6
            return _FakeIns()
        return f


class Prog:
    def __init__(self):
        self.ops = []
        self._pos = {}

    def _add(self, op, reads, writes):
        deps = {}
        odeps = {}
        pos = self._pos.get(op.eng, 0)
        self._pos[op.eng] = pos + 1
        op.pos = pos
        rd_rec, wr_rec = [], []
        for r in reads:
            (wr_rec if r.excl else rd_rec).append(r)
        for w in writes:
            wr_rec.append(w)

        def need(d):
            if d is None:
                return
            if d.dom == op.dom and not op.is_dma and op.eng == "pe":
                odeps[id(d)] = d
                return
            deps[id(d)] = d

        for r in rd_rec:
            need(r.w)
            for a in r.alias:
                need(a.w)
        for w in wr_rec:
            for x in (w,) + tuple(w.alias):
                if _MODEL_NOWAR and not x.excl and not x.alias and not x.name.startswith(_MODEL_KEEP) and (not _MODEL_DROP or x.name.startswith(_MODEL_DROP)):
                    continue
                if op.is_dma and x.w is not None and x.w.is_dma and x.w.dom == op.dom:
                    odeps[id(x.w)] = x.w
                else:
                    need(x.w)
                for dl in x.r.values():
                    for d in dl:
                        need(d)
        op.deps = tuple(deps.values())
        op.odeps = tuple(odeps.values())
        for r in rd_rec:
            lst = r.r.setdefault(op.dom, [])
            lst.append(op)
            if op.is_dma:
                del lst[:-1]
            else:
                while len(lst) > 1 and lst[0].pos < pos - 96:
                    lst.pop(0)
        for w in wr_rec:
            w.w = op
            w.r = {}
        op.idx = len(self.ops)
        self.ops.append(op)
        return op

    def op(self, eng, fn, reads=(), writes=(), c=None):
        if c is None:
            fe = _FakeEng(eng)
            fn(fe)
            c = fe.total
        return self._add(Op(eng, eng, fn, False, c, 0.0), reads, writes)

    def dma(self, fn, reads=(), writes=(), q="sp", dom="d0"):
        fe = _FakeEng("dma")
        fn(fe)
        issue = 1000.0 if q == "pool" else 100.0
        return self._add(Op(q, dom, fn, True, issue, 2000.0 + fe.total), reads, writes)

    def schedule(self, window=48, hop=120.0):
        per_eng = {}
        for o in self.ops:
            per_eng.setdefault(o.eng, []).append(o)
        order = {e: [] for e in per_eng}
        pos = {e: 0 for e in per_eng}
        pending = {e: list(l) for e, l in per_eng.items()}
        free = {e: 0.0 for e in per_eng}
        nleft = len(self.ops)
        while nleft:
            best = None
            for e, lst in pending.items():
                if not lst:
                    continue
                lim = window
                seen_dma = set()
                k = 0
                for o in lst:
                    if k >= lim:
                        break
                    k += 1
                    if o.is_dma:
                        if o.dom in seen_dma:
                            continue
                        seen_dma.add(o.dom)
                    rdy = 0.0
                    ok = True
                    for d in o.deps:
                        if d.done < 0:
                            ok = False
                            break
                        t = d.done + (hop if d.eng != e or d.is_dma else 60.0)
                        if t > rdy:
                            rdy = t
                    if ok:
                        for d in o.odeps:
                            if d.done < 0:
                                ok = False
                                break
                    if not ok:
                        continue
                    st = rdy if rdy > free[e] else free[e]
                    key = (st, o.idx)
                    if best is None or key < best[0]:
                        best = (key, e, o)
                    if rdy <= free[e]:
                        break
            if best is None:
                raise RuntimeError("scheduler deadlock")
            (st, _), e, o = best
            free[e] = st + o.cost
            o.done = st + o.cost + o.lat
            pending[e].remove(o)
            order[e].append(o)
            nleft -= 1
        self.est_ns = max(o.done for o in self.ops)
        return order

    def emit(self, block, sems, final_eng="sp", reorder=True):
        if reorder:
            order = self.schedule()
        else:
            order = {}
            for o in self.ops:
                order.setdefault(o.eng, []).append(o)
        for o in self.ops:
            for d in o.deps:
                d.signal = True
        cnt = {}
        for e, lst in order.items():
            for o in lst:
                if o.is_dma:
                    o.signal = True
                    cnt[o.dom] = cnt.get(o.dom, 0) + 16
                    o.count = cnt[o.dom]
                elif o.signal:
                    cnt[o.dom] = cnt.get(o.dom, 0) + 1
                    o.count = cnt[o.dom]
        final_counts = dict(cnt)
        engs = {"pe": "tensor", "act": "scalar", "dve": "vector", "pool": "gpsimd", "sp": "sync"}

        def make(olist, is_final):
            def body(e):
                seen = {}
                for o in olist:
                    need = {}
                    for d in o.deps:
                        if seen.get(d.dom, 0) >= d.count:
                            continue
                        if need.get(d.dom, 0) < d.count:
                            need[d.dom] = d.count
                    items = list(need.items())
                    for dom, c in items:
                        seen[dom] = c
                    att = None
                    if items and not o.is_dma:
                        att = items.pop()
                    for dom, c in items:
                        e.wait_ge(sems[dom], c)
                    if att is not None:
                        ins = o.fn(_AttachEng(e, sems[att[0]], att[1]))
                    else:
                        ins = o.fn(e)
                    if o.signal:
                        ins.then_inc(sems[o.dom], 16 if o.is_dma else 1)
                if is_final:
                    for dom, c in final_counts.items():
                        if seen.get(dom, 0) < c:
                            e.wait_ge(sems[dom], c)
            return body

        for ename in ("sp", "act", "pool", "dve", "pe"):
            olist = order.get(ename, [])
            is_final = ename == final_eng
            if not olist and not is_final:
                continue
            getattr(block, engs[ename])(make(olist, is_final))
        self.counts = final_counts


L = 2048
D = 1024
NT = 16
NL = 2
NCORES = 8
NSEQ = 6
FH = 2816
NJ = 22
ALPHA = float((2 * NL) ** 0.25)
EPS = 1e-5
MASKV = -30000.0
C_QA, C_KA, C_VA, C_RA, C_LR, C_QB, C_KB, C_VB = 0, 256, 512, 1024, 1536, 1568, 2080, 2592
NCST = 7 * 128 + 2


def _host_consts():
    s = np.arange(128)[:, None]
    t = np.arange(128)[None, :]
    same = (s // 64) == (t // 64)
    g = -1.0 / 16.0
    ident = np.eye(128, dtype=np.float32)
    triF = np.where(same & (s <= t), g, 0.0)
    triB = np.where(same & (s >= t), g, 0.0)
    triSF = np.where(same & (s > t), g, 0.0)
    triSB = np.where(same & (s < t), g, 0.0)
    mF = np.where(same & (s <= t), 1.0, 0.0)
    mB = np.where(same & (s > t), 1.0, 0.0)
    cind = np.where((np.arange(128)[:, None] // 64) == np.arange(2)[None, :], g, 0.0)
    return np.concatenate([ident, triF, triB, triSF, triSB, mF, mB, cind], axis=1).astype(np.float32)


def _nat_index_tables():
    p = np.arange(128)[:, None, None]
    j = np.arange(19)[None, :, None]
    q = np.arange(64)[None, None, :]
    kcol = p % 64
    half = p // 64
    cs = np.clip(q - 8, 0, 48)
    colok = (kcol >= cs) & (kcol < cs + 16)
    co = np.clip(kcol - q + 15, 0, 30)
    ro_even = j + half
    cc = j - 14
    kk = 2 * cc + half - 1
    ro_odd = kk + 3
    is_odd = j >= 14
    ro = np.where(is_odd, ro_odd, ro_even)
    rowok = np.where(is_odd, (kk >= 0) & (kk <= 7), ro_even <= 14)
    valid = colok & rowok
    ro = np.clip(ro, 0, 14)
    ro = np.broadcast_to(ro, (128, 19, 64))
    co = np.broadcast_to(co, (128, 19, 64))
    valid = np.broadcast_to(valid, (128, 19, 64))
    return ro, co, valid


def build(ns=NSEQ, nl=NL, taps=None):
    taps = taps or {}
    nc = bass.Bass("TRN2", target_bir_lowering=False)
    dram_in = lambda n, s: nc.dram_tensor(n, s, F32, kind="ExternalInput").ap()
    x_d = dram_in("x", [ns, L, D])
    w_in_d = dram_in("w_in", [NL, D, 3104])
    w_out_d = dram_in("w_out", [NL, D, D])
    w1_d = dram_in("w1", [NL, D, 2 * FH])
    w2_d = dram_in("w2", [NL, FH, D])
    w2g_d = dram_in("w2g", [NL, 33, 512])
    gn_d = dram_in("gn", [NL, 128, 512])
    lnp_d = dram_in("lnp", [NL, 4, 128, D])
    natb_d = dram_in("natb", [NL, 4, 128, 2 * 19 * 64])
    natm_d = dram_in("natm", [128, 19 * 64])
    cst_d = dram_in("cst", [128, NCST])
    y_d = nc.dram_tensor("y", [ns, L, D], F32, kind="ExternalOutput").ap()
    xs_d = nc.dram_tensor("xs_scr", [L, D], F32).ap()
    x1s_d = nc.dram_tensor("x1s_scr", [L, D], F32).ap()
    tap_d = {k: nc.dram_tensor(k, list(shp), F32, kind="ExternalOutput").ap() for k, shp in taps.items() if not k.startswith("_")}

    es = ExitStack()
    with es:
        sb = lambda n, s, d=F32: es.enter_context(nc.sbuf_tensor(n, s, d))
        psb = lambda n: es.enter_context(nc.psum_tensor(n, [128, 512], F32))
        arA = sb("arA", [128, 22656], BF16)
        arB = sb("arB", [128, 22784], BF16)
        XT = sb("XT", [128, 8, L], BF16)
        w1b = [sb(f"w1b{i}", [128, 8, 256], BF16) for i in range(2)]
        lnt = sb("lnt", [128, 2, D])
        natT = sb("natT", [128, 2, 19, 64])
        natM = sb("natM", [128, 19, 64])
        w2g = sb("w2g_sb", [33, 512])
        gnt = sb("gnt", [128, 512])
        cst = sb("cst_sb", [128, NCST])
        identb = sb("identb", [128, 128], BF16)
        lrT = sb("lrT", [33, 512], BF16)
        w2gb = sb("w2gb", [33, 512], BF16)
        qraw = sb("qraw", [128, 512])
        kraw = sb("kraw", [128, 512])
        ktm = sb("ktm", [128, 128])
        vbf = sb("vbf", [128, 256], BF16)
        rsb = sb("rsb", [128, 256])
        t1 = sb("t1", [128, 256])
        sp_ = sb("sp", [128, 256])
        EP = sb("EP", [128, 2, 128])
        EN = sb("EN", [128, 2, 128])
        EE = sb("EE", [128, 128])
        decb = sb("decb", [128, 2])
        qd = sb("qd", [128, 2, 128], BF16)
        ki = sb("ki", [128, 2, 128], BF16)
        ke = sb("ke", [128, 128], BF16)
        Asb = sb("Asb", [128, 256])
        Bsb = sb("Bsb", [128, 256])
        PT = sb("PT", [128, 2, 128], BF16)
        S32 = [sb(f"S32_{i}", [128, 128]) for i in range(2)]
        Sbf = [sb(f"Sbf_{i}", [128, 128], BF16) for i in range(2)]
        GR = sb("GR", [128, 256])
        ssq = sb("ssq", [128, 2])
        rstd = sb("rstd", [128, 2])
        junk = sb("junk", [128, 128])
        oa = sb("oa", [128, 256], BF16)
        Ssb = [sb(f"Ssb{i}", [128, 5, 64]) for i in range(4)]
        Pbf = [sb(f"Pbf{i}", [128, 5, 64], BF16) for i in range(4)]
        rcp = [sb(f"rcp{i}", [128, 2]) for i in range(2)]
        onb = [sb(f"onb{i}", [128, 128], BF16) for i in range(2)]
        xr = [sb(f"xr{i}", [128, D]) for i in range(3)]
        bst = sb("bst", [128, 2, 6])
        mv = sb("mv", [128, 2])
        lnr = sb("lnr", [128, 2])
        nmr = sb("nmr", [128, 1])
        xhb = [sb(f"xhb{i}", [128, D], BF16) for i in range(2)]
        sg = [sb(f"sg{i}", [128, 512]) for i in range(2)]
        PS = [psb(f"ps{i}") for i in range(8)]
        PSb = [p[:].bitcast(BF16) for p in PS]
        BK = [Res(f"bank{i}", excl=True) for i in range(8)]

        P = Prog()

        mixT = arA[:, 0:16384].rearrange("p (k t) -> p k t", k=8)
        qbT = arA[:, 16384:18432]
        kbT = arA[:, 18432:20480]
        vbA = arA[:, 20480:22560].rearrange("p (t h e) -> p t h e", t=16, h=2)
        act = arA[:, 0:22528].rearrange("p (j t) -> p j t", j=NJ)
        woutT = arB[:, 0:8192].rearrange("p (k c) -> p k c", k=8)
        wGf = arB[:, 8192:10496].rearrange("p (k c) -> p k c", k=8)
        wGt = arB[:, 10496:15616].rearrange("p (k c) -> p k c", k=8)
        wN = arB[:, 15616:18688].rearrange("p (k c) -> p k c", k=8)
        SbB = arB[:, 18688:22784].rearrange("p (n v) -> p n v", n=32)
        w2T = arB[:, 0:22528].rearrange("p (j c) -> p j c", j=NJ)

        class _TSet:
            pass
        _tspec = [("ktm", 128, F32), ("vbf", 256, BF16), ("rsb", 256, F32), ("t1", 256, F32), ("sp_", 256, F32), ("EP", 256, F32),
                  ("EN", 256, F32), ("EE", 128, F32), ("decb", 2, F32), ("qd", 256, BF16), ("ki", 256, BF16), ("ke", 128, BF16),
                  ("Asb", 256, F32), ("Bsb", 256, F32), ("PT", 256, BF16), ("GR", 256, F32), ("ssq", 2, F32), ("rstd", 2, F32),
                  ("junk", 128, F32), ("oa", 256, BF16)]
        _t0 = {"ktm": ktm, "vbf": vbf, "rsb": rsb, "t1": t1, "sp_": sp_, "EP": EP, "EN": EN, "EE": EE, "decb": decb, "qd": qd, "ki": ki,
               "ke": ke, "Asb": Asb, "Bsb": Bsb, "PT": PT, "GR": GR, "ssq": ssq, "rstd": rstd, "junk": junk, "oa": oa}
        TB = [_TSet(), _TSet()]
        TB[0].i, TB[1].i = 0, 1
        _off = 0
        _tres = {}
        for (nm_, n_, dt_) in _tspec:
            setattr(TB[0], nm_, _t0[nm_])
            w_ = n_ * (2 if dt_ == F32 else 1)
            v_ = arB[:, _off:_off + w_]
            if dt_ == F32:
                v_ = v_.bitcast(F32)
            if nm_ in ("EP", "EN", "qd", "ki", "PT"):
                v_ = v_.rearrange("p (d t) -> p d t", d=2)
            setattr(TB[1], nm_, v_)
            _tres[nm_] = (_off, _off + w_)
            _off += w_ + (w_ % 2)
        assert _off <= 8192

        RES = {}
        arena_lists = {"A": [], "B": []}

        def R(name, key=0, arena=None, lo=0, hi=0):
            k = (name, key)
            r = RES.get(k)
            if r is None:
                r = Res(f"{name}{key}")
                RES[k] = r
                if arena is not None:
                    r.lo, r.hi = lo, hi
                    al = []
                    for o in arena_lists[arena]:
                        if o.lo < hi and lo < o.hi:
                            al.append(o)
                            o.alias = tuple(o.alias) + (r,)
                    r.alias = tuple(al)
                    arena_lists[arena].append(r)
            return r

        def Rmix(kc, t):
            return R("mixT", (kc, t), "A", kc * 2048 + t * 128, kc * 2048 + t * 128 + 128)

        def Rqb(t):
            return R("qbT", t, "A", 16384 + t * 128, 16384 + t * 128 + 128)

        def Rkb(t):
            return R("kbT", t, "A", 18432 + t * 128, 18432 + t * 128 + 128)

        def Rvb(t):
            return R("vbA", t, "A", 20480 + t * 130, 20480 + t * 130 + 130)

        def Ract(j, blk):
            return R("act", (j, blk), "A", j * 1024 + blk * 512, j * 1024 + blk * 512 + 512)

        Rwout = lambda: R("wout", 0, "B", 0, 8192)
        RwGf = lambda: R("wGf", 0, "B", 8192, 10496)
        RwGt = lambda: R("wGt", 0, "B", 10496, 15616)
        RwN = lambda: R("wN", 0, "B", 15616, 18688)
        RSb = lambda n: R("SbB", n, "B", 18688 + n * 128, 18688 + n * 128 + 128)
        Rw2 = lambda j: R("w2T", j, "B", j * 1024, j * 1024 + 1024)
        RXT = lambda t: R("XT", t)
        _rn = {"sp": "sp_"}
        TB[0].R = lambda name: R(name + "_s0")
        TB[1].R = lambda name: R(name + "_s1", 0, "B", *_tres[_rn.get(name, name)])
        for kc in range(8):
            for t in range(NT):
                Rmix(kc, t)
        for t in range(NT):
            Rqb(t), Rkb(t), Rvb(t)
        for j in range(NJ):
            Ract(j, 0), Ract(j, 1), Rw2(j)
        Rwout(), RwGf(), RwGt(), RwN()
        for n in range(32):
            RSb(n)

        DSF = (PS[6][:, 0:128], PS[5][:, 384:512])
        c_ident = cst[:, 0:128]
        c_triF = cst[:, 128:256]
        c_triB = cst[:, 256:384]
        c_triSF = cst[:, 384:512]
        c_triSB = cst[:, 512:640]
        c_mF = cst[:, 640:768]
        c_mB = cst[:, 768:896]
        c_cind = cst[:, 896:898]

        P.dma(lambda e: e.dma_start(out=cst[:], in_=cst_d[:, :]), writes=[R("cst")], dom="l_cst")
        P.dma(lambda e: e.dma_start(out=natM[:].rearrange("p j q -> p (j q)"), in_=natm_d[:, :]), writes=[R("natM")], dom="l_natM")
        P.op("dve", lambda e: e.tensor_copy(out=identb[:], in_=c_ident), reads=[R("cst")], writes=[R("identb")])
        P.op("pool", lambda e: e.memset(lrT[32:33, :], 1.0), writes=[R("lrT1")])

        def tap(name, src_ap, res, dst=None):
            if name in tap_d:
                d = tap_d[name] if dst is None else dst
                P.dma(lambda e: e.dma_start(out=d, in_=src_ap), reads=res, dom="s_tap")

        def phase_x0(s):
            for t in reversed(range(NT)):
                b = t % 2
                P.dma(lambda e, t=t, b=b: e.dma_start(out=xhb[b][:], in_=x_d[s, t * 128:(t + 1) * 128, :]),
                      writes=[R("xhb", b)], q="pool", dom=f"l_xhb{b}")
                transposes_to_XT(xhb[b], R("xhb", b), t, 6 + b, None)

        def transposes_to_XT(src_bf, src_res, t, bank, scale_bias):
            def tr(e):
                ins = None
                for kc in range(8):
                    ins = e.transpose(out=PSb[bank][:, kc * 128:(kc + 1) * 128], in_=src_bf[:, kc * 128:(kc + 1) * 128],
                                      identity=identb[:])
                return ins
            P.op("pe", tr, reads=[src_res, R("identb")], writes=[BK[bank]])
            dst = XT[:, :, t * 128:(t + 1) * 128]
            src = PSb[bank][:, 0:1024].rearrange("p (k c) -> p k c", k=8)
            P.op("dve", lambda e: e.tensor_copy(out=dst, in_=src), reads=[BK[bank]], writes=[RXT(t)])

        def load_layer_tables(l):
            P.dma(lambda e: e.dma_start(out=w2g[:], in_=w2g_d[l, :, :]), writes=[R("w2g")], dom="l_w2g")
            P.op("dve", lambda e: e.tensor_copy(out=w2gb[:], in_=w2g[:]), reads=[R("w2g")], writes=[R("w2gb")])
            P.dma(lambda e: e.dma_start(out=gnt[:], in_=gn_d[l, :, :]), writes=[R("gnt")], dom="l_gnt")

        def gla_pass(s, l, p):
            wl = w_in_d[l].rearrange("(k q) c -> q k c", q=128)
            for (dst, c0, n) in ((wGf[:, :, 0:128], C_QA + 128 * p, 128), (wGf[:, :, 128:256], C_KA + 128 * p, 128),
                                 (wGf[:, :, 256:288], C_LR, 32)):
                P.dma(lambda e, dst=dst, c0=c0, n=n: e.dma_start(out=dst, in_=wl[:, :, c0:c0 + n]),
                      writes=[RwGf()], q="pool", dom="l_wGf")
            for (dst, c0, n) in ((wGt[:, :, 0:128], C_KA + 128 * p, 128), (wGt[:, :, 128:384], C_VA + 256 * p, 256),
                                 (wGt[:, :, 384:640], C_RA + 256 * p, 256)):
                P.dma(lambda e, dst=dst, c0=c0, n=n: e.dma_start(out=dst, in_=wl[:, :, c0:c0 + n]),
                      writes=[RwGt()], q="pool", dom="l_wGt")
            gf0 = 128 * p
            gb0 = 256 + 128 * p

            def proj_fm(blk, c0, n, bank, m0=0):
                def f(e):
                    ins = None
                    for kc in range(8):
                        ins = e.matmul(PS[bank][m0:m0 + n, 0:512], lhsT=wGf[:, kc, c0:c0 + n],
                                       rhs=XT[:, kc, blk * 512:(blk + 1) * 512], start=(kc == 0), stop=(kc == 7))
                    return ins
                P.op("pe", f, reads=[RwGf()] + [RXT(4 * blk + i) for i in range(4)], writes=[BK[bank]])

            def proj_tm(t, c0, n, bank):
                def f(e):
                    ins = None
                    for kc in range(8):
                        ins = e.matmul(PS[bank][:, 0:n], lhsT=XT[:, kc, t * 128:(t + 1) * 128], rhs=wGt[:, kc, c0:c0 + n],
                                       start=(kc == 0), stop=(kc == 7))
                    return ins
                P.op("pe", f, reads=[RwGt(), RXT(t)], writes=[BK[bank]])

            def lr_block(blk):
                proj_fm(blk, 256, 32, 2)
                P.op("act", lambda e: e.copy(out=lrT[0:32, :], in_=PS[2][0:32, 0:512]), reads=[BK[2]], writes=[R("lrT")])

            def softplus_neg(ncols, B):
                P.op("act", lambda e: e.activation(out=B.t1[:, 0:ncols], in_=PS[2][:, 0:ncols], func=AF.Exp, scale=-1.0),
                     reads=[BK[2]], writes=[B.R("t1")])
                P.op("act", lambda e: e.activation(out=B.sp_[:, 0:ncols], in_=B.t1[:, 0:ncols], func=AF.Ln, bias=1.0),
                     reads=[B.R("t1")], writes=[B.R("sp")])

            def tile_a(t, ti, blk, cur, B):
                t = 4 * blk + ti
                proj_tm(t, 0, 384, 3)
                P.op("act", lambda e: e.copy(out=B.ktm[:], in_=PS[3][:, 0:128]), reads=[BK[3]], writes=[B.R("ktm")])
                P.op("dve", lambda e: e.tensor_copy(out=B.vbf[:], in_=PS[3][:, 128:384]), reads=[BK[3]], writes=[B.R("vbf")])
                ck("A2")
                P.op("pe", lambda e, ti=ti: e.matmul(PS[2][:, 0:128], lhsT=lrT[0:33, ti * 128:(ti + 1) * 128],
                                                     rhs=w2gb[0:33, gb0:gb0 + 128], start=True, stop=True),
                     reads=[R("lrT"), R("lrT1"), R("w2gb")], writes=[BK[2]])
                softplus_neg(128, B)
                ck("A3")

                def cums(e):
                    e.matmul(PS[5][:, 0:128], lhsT=c_triSB, rhs=B.sp_[:, 0:128], start=True, stop=True)
                    return e.matmul(PS[5][:, 128:130], lhsT=B.sp_[:, 0:128], rhs=c_cind, start=True, stop=True)
                P.op("pe", cums, reads=[B.R("sp"), R("cst")], writes=[BK[5]])
                P.op("act", lambda e: e.activation(out=B.EE[:], in_=PS[5][:, 0:128], func=AF.Exp), reads=[BK[5]], writes=[B.R("EE")])
                P.op("act", lambda e: e.activation(out=B.decb[:], in_=PS[5][:, 128:130], func=AF.Exp), reads=[BK[5]], writes=[B.R("decb")])
                P.op("dve", lambda e: e.tensor_tensor(out=B.ke[:], in0=B.ktm[:], in1=B.EE[:], op=ALU.mult),
                     reads=[B.R("ktm"), B.R("EE")], writes=[B.R("ke")])

                def dsb(e):
                    ins = None
                    for j in range(2):
                        for hh in range(2):
                            ins = e.matmul(PS[6 + j][64 * hh:64 * hh + 64, 0:128],
                                           lhsT=B.ke[64 * j:64 * j + 64, 64 * hh:64 * hh + 64],
                                           rhs=B.vbf[64 * j:64 * j + 64, 128 * hh:128 * hh + 128],
                                           start=True, stop=True, tile_position=(64 * j, 64 * hh))
                    return ins
                ck("A4")
                P.op("pe", dsb, reads=[B.R("ke"), B.R("vbf")], writes=[BK[6], BK[7]])
                ck("A5")
                for j in (1, 0):
                    n = 2 * t + j
                    P.op("pool", lambda e, n=n, cur=cur: e.tensor_copy(out=SbB[:, n, :], in_=S32[cur][:]),
                         reads=[R("S32", cur)], writes=[RSb(n)])
                    P.op("dve", lambda e, j=j, cur=cur: e.scalar_tensor_tensor(
                        out=S32[1 - cur][:], in0=S32[cur][:], scalar=B.decb[:, j:j + 1], in1=PS[6 + j][:, 0:128],
                        op0=ALU.mult, op1=ALU.add),
                        reads=[R("S32", cur), B.R("decb"), BK[6 + j]], writes=[R("S32", 1 - cur)])
                    cur = 1 - cur
                return cur

            def tile_b(t, ti, blk, cur, B):
                t = 4 * blk + ti
                tc0 = ti * 128
                proj_tm(t, 0, 384, 3)
                proj_tm(t, 384, 256, 4)
                P.op("act", lambda e: e.copy(out=B.ktm[:], in_=PS[3][:, 0:128]), reads=[BK[3]], writes=[B.R("ktm")])
                P.op("dve", lambda e: e.tensor_copy(out=B.vbf[:], in_=PS[3][:, 128:384]), reads=[BK[3]], writes=[B.R("vbf")])
                P.op("act", lambda e: e.activation(out=B.rsb[:], in_=PS[4][:, 0:256], func=AF.Exp, scale=-1.0), reads=[BK[4]], writes=[B.R("rsb")])
                P.op("act", lambda e: e.activation(out=B.rsb[:], in_=B.rsb[:], func=AF.Ln, bias=1.0), reads=[B.R("rsb")], writes=[B.R("rsb")])
                P.op("act", lambda e: e.activation(out=B.rsb[:], in_=B.rsb[:], func=AF.Exp, scale=-1.0), reads=[B.R("rsb")], writes=[B.R("rsb")])
                P.op("pool", lambda e: e.tensor_tensor(out=B.rsb[:], in0=B.rsb[:], in1=gnt[:, 256 * p:256 * p + 256], op=ALU.mult),
                     reads=[B.R("rsb"), R("gnt")], writes=[B.R("rsb")])
                P.op("dve", lambda e: e.tensor_tensor(out=B.GR[:], in0=PS[4][:, 0:256], in1=B.rsb[:], op=ALU.mult),
                     reads=[BK[4], B.R("rsb")], writes=[B.R("GR")])

                def zmm(e, tc0=tc0):
                    e.matmul(PS[2][:, 0:128], lhsT=lrT[0:33, tc0:tc0 + 128], rhs=w2gb[0:33, gf0:gf0 + 128], start=True, stop=True)
                    return e.matmul(PS[2][:, 128:256], lhsT=lrT[0:33, tc0:tc0 + 128], rhs=w2gb[0:33, gb0:gb0 + 128], start=True, stop=True)
                P.op("pe", zmm, reads=[R("lrT"), R("lrT1"), R("w2gb")], writes=[BK[2]])
                softplus_neg(256, B)

                def cums2(e):
                    e.matmul(PS[5][:, 0:128], lhsT=B.sp_[:, 0:128], rhs=c_triF, start=True, stop=True)
                    e.matmul(PS[5][:, 128:256], lhsT=B.sp_[:, 128:256], rhs=c_triB, start=True, stop=True)
                    return e.matmul(PS[5][:, 256:384], lhsT=c_triSF, rhs=B.sp_[:, 0:128], start=True, stop=True)
                P.op("pe", cums2, reads=[B.R("sp"), R("cst")], writes=[BK[5]])
                bc = PS[5][:, 0:256].rearrange("p (d t) -> p d t", d=2)
                P.op("act", lambda e, bc=bc: e.activation(out=B.EP[:], in_=bc, func=AF.Exp), reads=[BK[5]], writes=[B.R("EP")])
                P.op("act", lambda e, bc=bc: e.activation(out=B.EN[:], in_=bc, func=AF.Exp, scale=-1.0), reads=[BK[5]], writes=[B.R("EN")])
                P.op("act", lambda e: e.activation(out=B.EE[:], in_=PS[5][:, 256:384], func=AF.Exp), reads=[BK[5]], writes=[B.R("EE")])
                for d in range(2):
                    P.op("dve", lambda e, d=d, tc0=tc0: e.scalar_tensor_tensor(
                        out=B.qd[:, d, :], in0=qraw[:, tc0:tc0 + 128], scalar=0.125, in1=B.EP[:, d, :], op0=ALU.mult, op1=ALU.mult),
                        reads=[R("qraw"), B.R("EP")], writes=[B.R("qd")])
                    P.op("dve", lambda e, d=d, tc0=tc0: e.tensor_tensor(out=B.ki[:, d, :], in0=kraw[:, tc0:tc0 + 128], in1=B.EN[:, d, :], op=ALU.mult),
                         reads=[R("kraw"), B.R("EN")], writes=[B.R("ki")])
                P.op("dve", lambda e: e.tensor_tensor(out=B.ke[:], in0=B.ktm[:], in1=B.EE[:], op=ALU.mult),
                     reads=[B.R("ktm"), B.R("EE")], writes=[B.R("ke")])

                def scores(e):
                    ins = None
                    for d in range(2):
                        for hh in range(2):
                            ins = e.matmul(PS[hh][:, d * 128:(d + 1) * 128],
                                           lhsT=B.ki[64 * hh:64 * hh + 64, d, :], rhs=B.qd[64 * hh:64 * hh + 64, d, :],
                                           start=True, stop=True, tile_position=(64 * hh, 0))
                    return ins
                P.op("pe", scores, reads=[B.R("ki"), B.R("qd")], writes=[BK[0], BK[1]])
                for hh in range(2):
                    P.op("dve", lambda e, hh=hh: e.tensor_tensor(out=B.Asb[:, hh * 128:(hh + 1) * 128], in0=PS[hh][:, 0:128],
                                                              in1=c_mF, op=ALU.mult), reads=[BK[hh], R("cst")], writes=[B.R("Asb")])
                    P.op("dve", lambda e, hh=hh: e.tensor_tensor(out=B.Bsb[:, hh * 128:(hh + 1) * 128], in0=PS[hh][:, 128:256],
                                                              in1=c_mB, op=ALU.mult), reads=[BK[hh], R("cst")], writes=[B.R("Bsb")])
                P.op("pool", lambda e: e.tensor_tensor(out=B.PT[:].rearrange("p h c -> p (h c)"), in0=B.Asb[:], in1=B.Bsb[:], op=ALU.add),
                     reads=[B.R("Asb"), B.R("Bsb")], writes=[B.R("PT")])

                def dsf(e):
                    ins = None
                    for j in range(2):
                        for hh in range(2):
                            ins = e.matmul(DSF[j][64 * hh:64 * hh + 64, :],
                                           lhsT=B.ke[64 * j:64 * j + 64, 64 * hh:64 * hh + 64],
                                           rhs=B.vbf[64 * j:64 * j + 64, 128 * hh:128 * hh + 128],
                                           start=True, stop=True, tile_position=(64 * j, 64 * hh))
                    return ins
                P.op("pe", dsf, reads=[B.R("ke"), B.R("vbf")], writes=[BK[6], BK[5]])
                c0 = cur
                for j in range(2):
                    P.op("dve", lambda e, j=j, cur=cur: e.scalar_tensor_tensor(
                        out=S32[1 - cur][:], in0=S32[cur][:], scalar=B.EP[:, 0, 64 * j + 63:64 * j + 64],
                        in1=DSF[j][:, :], op0=ALU.mult, op1=ALU.add),
                        reads=[R("S32", cur), B.R("EP"), BK[(6, 5)[j]]], writes=[R("S32", 1 - cur)])
                    cur = 1 - cur
                    if j == 0:
                        P.op("pool", lambda e, cur=cur: e.tensor_copy(out=Sbf[1][:], in_=S32[cur][:]), reads=[R("S32", cur)], writes=[R("Sbf", 1)])
                def omm(e, t=t):
                    ins = None
                    for hh in range(2):
                        o_ = PS[7][:, hh * 128:(hh + 1) * 128]
                        e.matmul(o_, lhsT=B.PT[:, hh, :], rhs=B.vbf[:, 128 * hh:128 * hh + 128], start=True, stop=False)
                        for j in range(2):
                            n = 2 * t + j
                            oj = PS[7][64 * j:64 * j + 64, hh * 128:(hh + 1) * 128]
                            e.matmul(oj, lhsT=B.qd[64 * hh:64 * hh + 64, 0, 64 * j:64 * j + 64], rhs=Sbf[j][64 * hh:64 * hh + 64, :],
                                     start=False, stop=False, tile_position=(64 * hh, 64 * j))
                            ins = e.matmul(oj, lhsT=B.qd[64 * hh:64 * hh + 64, 1, 64 * j:64 * j + 64], rhs=SbB[64 * hh:64 * hh + 64, n, :],
                                           start=False, stop=True, tile_position=(64 * hh, 64 * j))
                    return ins
                P.op("pe", omm, reads=[B.R("PT"), B.R("vbf"), B.R("qd"), R("Sbf", 0), R("Sbf", 1), RSb(2 * t), RSb(2 * t + 1)], writes=[BK[7]])
                P.op("pool", lambda e, cur=cur: e.tensor_copy(out=Sbf[0][:], in_=S32[cur][:]), reads=[R("S32", cur)], writes=[R("Sbf", 0)])
                for hh in range(2):
                    P.op("act", lambda e, hh=hh: e.activation(out=B.junk[:], in_=PS[7][:, hh * 128:(hh + 1) * 128], func=AF.Square,
                                                             accum_out=B.ssq[:, hh:hh + 1]), reads=[BK[7]], writes=[B.R("ssq"), B.R("junk")])
                P.op("act", lambda e: e.activation(out=B.rstd[:], in_=B.ssq[:], func=AF.Ln, scale=1.0 / 128.0, bias=EPS), reads=[B.R("ssq")], writes=[B.R("rstd")])
                P.op("act", lambda e: e.activation(out=B.rstd[:], in_=B.rstd[:], func=AF.Exp, scale=-0.5), reads=[B.R("rstd")], writes=[B.R("rstd")])
                for hh in range(2):
                    P.op("dve", lambda e, hh=hh: e.scalar_tensor_tensor(
                        out=B.oa[:, hh * 128:(hh + 1) * 128], in0=PS[7][:, hh * 128:(hh + 1) * 128], scalar=B.rstd[:, hh:hh + 1],
                        in1=B.GR[:, hh * 128:(hh + 1) * 128], op0=ALU.mult, op1=ALU.mult),
                        reads=[BK[7], B.R("rstd"), B.R("GR")], writes=[B.R("oa")])

                def otr(e):
                    e.transpose(out=PSb[7][:, 512:640], in_=B.oa[:, 0:128], identity=identb[:])
                    return e.transpose(out=PSb[7][:, 640:768], in_=B.oa[:, 128:256], identity=identb[:])
                P.op("pe", otr, reads=[B.R("oa"), R("identb")], writes=[BK[7]])
                P.op("dve", lambda e, t=t: e.tensor_copy(out=mixT[:, 2 * p:2 * p + 2, t * 128:(t + 1) * 128],
                                                       in_=PSb[7][:, 512:768].rearrange("p (k c) -> p k c", k=2)),
                     reads=[BK[7]], writes=[Rmix(2 * p, t), Rmix(2 * p + 1, t)])
                return cur

            P.op("pool", lambda e: e.memset(S32[0][:], 0.0), writes=[R("S32", 0)])
            cur = 0
            gs = taps.get("_gstop")

            def ck(name):
                if gs == name:
                    raise _Stop()
            for blk in (3, 2, 1, 0):
                ck("A0")
                lr_block(blk)
                ck("A1")
                for ti in (3, 2, 1, 0):
                    cur = tile_a(4 * blk + ti, ti, blk, cur, TB[ti % 2])
            if taps.get("_gstop") == "A":
                return
            P.op("pool", lambda e: e.memset(S32[0][:], 0.0), writes=[R("S32", 0)])
            P.op("pool", lambda e: e.memset(Sbf[0][:], 0.0), writes=[R("Sbf", 0)])
            cur = 0
            for blk in range(4):
                proj_fm(blk, 0, 128, 0)
                proj_fm(blk, 128, 128, 1)
                lr_block(blk)
                P.op("act", lambda e: e.copy(out=qraw[:], in_=PS[0][:, :]), reads=[BK[0]], writes=[R("qraw")])
                P.op("dve", lambda e: e.tensor_copy(out=kraw[:], in_=PS[1][:, :]), reads=[BK[1]], writes=[R("kraw")])
                for ti in range(4):
                    cur = tile_b(4 * blk + ti, ti, blk, cur, TB[ti % 2])

        def nat_pass(s, l, c):
            wl = w_in_d[l].rearrange("(k q) c -> q k c", q=128)
            for (dst, c0) in ((wN[:, :, 0:128], C_QB + 128 * c), (wN[:, :, 128:256], C_KB + 128 * c), (wN[:, :, 256:384], C_VB + 128 * c)):
                P.dma(lambda e, dst=dst, c0=c0: e.dma_start(out=dst, in_=wl[:, :, c0:c0 + 128]), writes=[RwN()], q="pool", dom="l_wN")
            P.dma(lambda e: e.dma_start(out=natT[:].rearrange("p h j q -> p (h j q)"), in_=natb_d[l, c, :, :]), writes=[R("natT")], dom="l_natT")
            for hh in range(2):
                P.op("pool", lambda e, hh=hh: e.tensor_tensor(out=natT[:, hh, :, :], in0=natT[:, hh, :, :], in1=natM[:], op=ALU.add),
                     reads=[R("natT"), R("natM")], writes=[R("natT")])
            for i in range(2):
                P.op("pool", lambda e, i=i: e.memset(vbA[:, :, i, 64:65], 1.0), writes=[Rvb(t) for t in range(NT)])
            for blk in range(4):
                for (c0, bank) in ((0, 0), (128, 1)):
                    def f(e, c0=c0, bank=bank, blk=blk):
                        ins = None
                        for kc in range(8):
                            ins = e.matmul(PS[bank][:, 0:512], lhsT=wN[:, kc, c0:c0 + 128], rhs=XT[:, kc, blk * 512:(blk + 1) * 512],
                                           start=(kc == 0), stop=(kc == 7))
                        return ins
                    P.op("pe", f, reads=[RwN()] + [RXT(4 * blk + i) for i in range(4)], writes=[BK[bank]])
                P.op("act", lambda e, blk=blk: e.mul(out=qbT[:, blk * 512:(blk + 1) * 512], in_=PS[0][:, :], mul=0.125),
                     reads=[BK[0]], writes=[Rqb(4 * blk + i) for i in range(4)])
                P.op("dve", lambda e, blk=blk: e.tensor_copy(out=kbT[:, blk * 512:(blk + 1) * 512], in_=PS[1][:, :]),
                     reads=[BK[1]], writes=[Rkb(4 * blk + i) for i in range(4)])
                for ti in range(4):
                    t = 4 * blk + ti
                    vbk = (2, 7, 3, 4)[ti]

                    def fv(e, t=t, vbk=vbk):
                        ins = None
                        for kc in range(8):
                            ins = e.matmul(PS[vbk][:, 0:128], lhsT=XT[:, kc, t * 128:(t + 1) * 128], rhs=wN[:, kc, 256:384],
                                           start=(kc == 0), stop=(kc == 7))
                        return ins
                    P.op("pe", fv, reads=[RwN(), RXT(t)], writes=[BK[vbk]])
                    P.op("act" if ti % 2 == 0 else "dve", (lambda e, t=t, vbk=vbk: e.copy(out=vbA[:, t, :, 0:64], in_=PS[vbk][:, 0:128].rearrange("p (h d) -> p h d", h=2)))
                         if ti % 2 == 0 else (lambda e, t=t, vbk=vbk: e.tensor_copy(out=vbA[:, t, :, 0:64], in_=PS[vbk][:, 0:128].rearrange("p (h d) -> p h d", h=2))),
                         reads=[BK[vbk]], writes=[Rvb(t)])
            it = 0
            for t in range(NT):
                for rr in range(2):
                    r = 2 * t + rr
                    rs = min(max(r - 4, 0), 24)
                    if rs % 2 == 0:
                        a0, nch = rs // 2, 4
                        jb = [(rs - r + 7) + 2 * cc for cc in range(4)]
                        tab = lambda hh, jb=jb: natT[:, hh, jb[0]:jb[0] + 7:2, :]
                    else:
                        a0, nch = (rs - 1) // 2, 5
                        tab = lambda hh: natT[:, hh, 14:19, :]
                    for hh in range(2):
                        bank = 3 + (it % 4)
                        sbi = it % 4
                        it += 1

                        def smm(e, a0=a0, nch=nch, hh=hh, bank=bank, r=r):
                            ins = None
                            for cc in range(nch):
                                a = a0 + cc
                                ins = e.matmul(PS[bank][:, cc * 64:(cc + 1) * 64], lhsT=kbT[64 * hh:64 * hh + 64, a * 128:(a + 1) * 128],
                                               rhs=qbT[64 * hh:64 * hh + 64, r * 64:(r + 1) * 64], start=True, stop=True)
                            return ins
                        P.op("pe", smm, reads=[Rkb(a0 + cc) for cc in range(nch)] + [Rqb(t)], writes=[BK[bank]])
                        psv = PS[bank][:, 0:nch * 64].rearrange("p (c q) -> p c q", c=nch)
                        P.op("dve", lambda e, psv=psv, nch=nch, sbi=sbi, tab=tab, hh=hh: e.tensor_tensor(
                            out=Ssb[sbi][:, 0:nch, :], in0=psv, in1=tab(hh), op=ALU.add),
                            reads=[BK[bank], R("natT")], writes=[R("Ssb", sbi)])
                        P.op("act", lambda e, nch=nch, sbi=sbi: e.activation(out=Pbf[sbi][:, 0:nch, :], in_=Ssb[sbi][:, 0:nch, :], func=AF.Exp),
                             reads=[R("Ssb", sbi)], writes=[R("Pbf", sbi)])

                        def pv(e, a0=a0, nch=nch, hh=hh, sbi=sbi, rr=rr):
                            ins = None
                            for cc in range(nch):
                                a = a0 + cc
                                ins = e.matmul(PS[7][64 * rr:64 * rr + 64, hh * 65:hh * 65 + 65], lhsT=Pbf[sbi][:, cc, :],
                                               rhs=vbA[:, a, hh, :], start=(cc == 0), stop=(cc == nch - 1), tile_position=(0, 64 * rr))
                            return ins
                        P.op("pe", pv, reads=[R("Pbf", sbi)] + [Rvb(a0 + cc) for cc in range(nch)], writes=[BK[7]])
                ov = PS[7][:, 0:130].rearrange("p (h e) -> p h e", h=2)
                tb_ = t % 2
                P.op("dve", lambda e, ov=ov, tb_=tb_: e.reciprocal(out=rcp[tb_][:].rearrange("p (h o) -> p h o", o=1), in_=ov[:, :, 64:65]),
                     reads=[BK[7]], writes=[R("rcp", tb_)])
                for hh in range(2):
                    P.op("dve", lambda e, hh=hh, tb_=tb_: e.tensor_scalar(out=onb[tb_][:, hh * 64:(hh + 1) * 64], in0=PS[7][:, hh * 65:hh * 65 + 64],
                                                               scalar1=rcp[tb_][:, hh:hh + 1], scalar2=None, op0=ALU.mult),
                         reads=[BK[7], R("rcp", tb_)], writes=[R("onb", tb_)])
                P.op("pe", lambda e, tb_=tb_: e.transpose(out=PSb[2][:, 512:640], in_=onb[tb_][:], identity=identb[:]), reads=[R("onb", tb_), R("identb")], writes=[BK[2]])
                P.op("act", lambda e, t=t: e.copy(out=mixT[:, 4 + c, t * 128:(t + 1) * 128], in_=PSb[2][:, 512:640]),
                     reads=[BK[2]], writes=[Rmix(4 + c, t)])

        def layer_norm_tile(buf, bres, gi):
            xb = xr[buf]

            def st(e):
                e.bn_stats(out=bst[:, 0, :], in_=xb[:, 0:512])
                return e.bn_stats(out=bst[:, 1, :], in_=xb[:, 512:1024])
            P.op("dve", st, reads=[bres], writes=[R("bst")])
            P.op("dve", lambda e: e.bn_aggr(out=mv[:], in_=bst[:]), reads=[R("bst")], writes=[R("mv")])
            P.op("act", lambda e: e.activation(out=lnr[:, 0:1], in_=mv[:, 1:2], func=AF.Ln, bias=EPS), reads=[R("mv")], writes=[R("lnr")])
            P.op("act", lambda e: e.activation(out=lnr[:, 1:2], in_=lnr[:, 0:1], func=AF.Exp, scale=-0.5), reads=[R("lnr")], writes=[R("lnr")])
            P.op("dve", lambda e: e.scalar_tensor_tensor(out=xb[:], in0=xb[:], scalar=mv[:, 0:1], in1=lnt[:, 0, :], op0=ALU.subtract, op1=ALU.mult),
                 reads=[bres, R("mv"), R("lnt")], writes=[bres])
            P.op("act", lambda e: e.activation(out=xb[:], in_=xb[:], func=AF.Copy, scale=lnr[:, 1:2]), reads=[bres, R("lnr")], writes=[bres])
            P.op("pool", lambda e: e.tensor_tensor(out=xb[:], in0=xb[:], in1=lnt[:, 1, :], op=ALU.add), reads=[bres, R("lnt")], writes=[bres])

        def phase_o(s, l):
            wl = w_out_d[l].rearrange("(k q) c -> q k c", q=128)
            for h in range(2):
                P.dma(lambda e, h=h: e.dma_start(out=woutT[:, :, h * 512:(h + 1) * 512], in_=wl[:, :, h * 512:(h + 1) * 512]),
                      writes=[Rwout()], q="pool", dom="l_wout")
            P.dma(lambda e: e.dma_start(out=lnt[:], in_=lnp_d[l, 0:2].rearrange("g p d -> p g d")), writes=[R("lnt")], dom="l_lnt")
            src = x_d[s] if l == 0 else xs_d
            for t in range(NT):
                b = t % 3
                bres = R("xr", b)
                P.dma(lambda e, t=t, b=b: e.dma_start(out=xr[b][:], in_=src[t * 128:(t + 1) * 128, :]),
                      reads=([R("xs", t)] if l > 0 else []), writes=[bres], dom=f"l_xr{b}")
                bk = 2 * (t % 2)
                for h in range(2):
                    def f(e, h=h, t=t, bk=bk):
                        ins = None
                        for kc in range(8):
                            ins = e.matmul(PS[bk + h][:, :], lhsT=mixT[:, kc, t * 128:(t + 1) * 128], rhs=woutT[:, kc, h * 512:(h + 1) * 512],
                                           start=(kc == 0), stop=(kc == 7))
                        return ins
                    P.op("pe", f, reads=[Rwout()] + [Rmix(kc, t) for kc in range(8)], writes=[BK[bk + h]])
                    P.op("dve", lambda e, h=h, b=b, bk=bk: e.scalar_tensor_tensor(
                        out=xr[b][:, h * 512:(h + 1) * 512], in0=xr[b][:, h * 512:(h + 1) * 512], scalar=ALPHA, in1=PS[bk + h][:, :],
                        op0=ALU.mult, op1=ALU.add), reads=[bres, BK[bk + h]], writes=[bres])
                layer_norm_tile(b, bres, 0)
                P.dma(lambda e, t=t, b=b: e.dma_start(out=x1s_d[t * 128:(t + 1) * 128, :], in_=xr[b][:]), reads=[bres], writes=[R("x1s", t)], dom=f"s_xr{b}")
                hb = t % 2
                P.op("act", lambda e, b=b, hb=hb: e.copy(out=xhb[hb][:], in_=xr[b][:]), reads=[bres], writes=[R("xhb", hb)])
                transposes_to_XT(xhb[hb], R("xhb", hb), t, 4 + hb, None)

        def phase_f(s, l):
            w1l = w1_d[l].rearrange("(k q) c -> q k c", q=128)
            w2l = w2_d[l].rearrange("(j q) c -> q j c", q=128)
            for j0 in range(0, NJ, 6):
                j1 = min(NJ, j0 + 6)
                P.dma(lambda e, j0=j0, j1=j1: e.dma_start(out=w2T[:, j0:j1, :], in_=w2l[:, j0:j1, :]),
                      writes=[Rw2(j) for j in range(j0, j1)], q="pool", dom=f"l_w2_{j0}")
            P.dma(lambda e: e.dma_start(out=lnt[:], in_=lnp_d[l, 2:4].rearrange("g p d -> p g d")), writes=[R("lnt")], dom="l_lnt")
            it = 0
            for half in range(2):
                for j in range(NJ):
                    wb = j % 2
                    P.dma(lambda e, j=j, wb=wb: e.dma_start(out=w1b[wb][:, :, 0:128], in_=w1l[:, :, j * 128:(j + 1) * 128]),
                          writes=[R("w1b", wb)], q="pool", dom=f"l_w1b{wb}")
                    P.dma(lambda e, j=j, wb=wb: e.dma_start(out=w1b[wb][:, :, 128:256], in_=w1l[:, :, FH + j * 128:FH + (j + 1) * 128]),
                          writes=[R("w1b", wb)], q="pool", dom=f"l_w1b{wb}")
                    for bl in range(2):
                        tb = half * 2 + bl
                        gb, ub = (0, 1) if it % 2 == 0 else (2, 3)
                        sgi = it % 2
                        it += 1
                        for (bank, c0) in ((gb, 0), (ub, 128)):
                            def f(e, bank=bank, c0=c0, wb=wb, tb=tb):
                                ins = None
                                for kc in range(8):
                                    ins = e.matmul(PS[bank][:, :], lhsT=w1b[wb][:, kc, c0:c0 + 128], rhs=XT[:, kc, tb * 512:(tb + 1) * 512],
                                                   start=(kc == 0), stop=(kc == 7))
                                return ins
                            P.op("pe", f, reads=[R("w1b", wb)] + [RXT(4 * tb + i) for i in range(4)], writes=[BK[bank]])
                        P.op("act", lambda e, gb=gb, sgi=sgi: e.activation(out=sg[sgi][:], in_=PS[gb][:, :], func=AF.Silu),
                             reads=[BK[gb]], writes=[R("sg", sgi)])
                        P.op("dve", lambda e, ub=ub, sgi=sgi, j=j, bl=bl: e.tensor_tensor(
                            out=act[:, j, bl * 512:(bl + 1) * 512], in0=PS[ub][:, :], in1=sg[sgi][:], op=ALU.mult),
                            reads=[BK[ub], R("sg", sgi)], writes=[Ract(j, bl)])
                for tt in range(8):
                    t = half * 8 + tt
                    b = t % 3
                    bres = R("xr", b)
                    P.dma(lambda e, t=t, b=b: e.dma_start(out=xr[b][:], in_=x1s_d[t * 128:(t + 1) * 128, :]),
                          reads=[R("x1s", t)], writes=[bres], dom=f"l_xr{b}")
                    bk = 4 + 2 * (t % 2)
                    for h in range(2):
                        def f(e, h=h, tt=tt, bk=bk):
                            ins = None
                            for j in range(NJ):
                                ins = e.matmul(PS[bk + h][:, :], lhsT=act[:, j, tt * 128:(tt + 1) * 128], rhs=w2T[:, j, h * 512:(h + 1) * 512],
                                               start=(j == 0), stop=(j == NJ - 1))
                            return ins
                        P.op("pe", f, reads=[Rw2(j) for j in range(NJ)] + [Ract(j, tt // 4) for j in range(NJ)], writes=[BK[bk + h]])
                        P.op("dve", lambda e, h=h, b=b, bk=bk: e.scalar_tensor_tensor(
                            out=xr[b][:, h * 512:(h + 1) * 512], in0=xr[b][:, h * 512:(h + 1) * 512], scalar=ALPHA, in1=PS[bk + h][:, :],
                            op0=ALU.mult, op1=ALU.add), reads=[bres, BK[bk + h]], writes=[bres])
                    layer_norm_tile(b, bres, 2)
                    if l == nl - 1:
                        P.dma(lambda e, t=t, b=b: e.dma_start(out=y_d[s, t * 128:(t + 1) * 128, :], in_=xr[b][:]), reads=[bres], dom=f"s_xr{b}")
                    else:
                        P.dma(lambda e, t=t, b=b: e.dma_start(out=xs_d[t * 128:(t + 1) * 128, :], in_=xr[b][:]), reads=[bres],
                              writes=[R("xs", t)], dom=f"s_xr{b}")
                        hb = t % 2
                        P.op("act", lambda e, b=b, hb=hb: e.copy(out=xhb[hb][:], in_=xr[b][:]), reads=[bres], writes=[R("xhb", hb)])
                        transposes_to_XT(xhb[hb], R("xhb", hb), t, hb, None)

        phases = taps.get("_phases", ("x0", "gla", "nat", "o", "f"))
        for s in range(ns):
            if "x0" in phases:
                phase_x0(s)
            for l in range(nl):
                load_layer_tables(l)
                if "gla" in phases:
                    try:
                        for p in range(2):
                            gla_pass(s, l, p)
                    except _Stop:
                        pass
                if "nat" in phases:
                    for c in range(4):
                        nat_pass(s, l, c)
                if "mix" in tap_d and s == 0 and l == 0:
                    for kc in ([0, 1, 2, 3] if "gla" in phases else []) + ([4, 5, 6, 7] if "nat" in phases else []):
                        P.op("dve", lambda e, kc=kc: e.tensor_copy(out=xr[0][:, 0:1024], in_=mixT[:, kc, 0:1024]),
                             reads=[Rmix(kc, t) for t in range(8)], writes=[R("xr", 0)])
                        P.dma(lambda e, kc=kc: e.dma_start(out=tap_d["mix"][kc, :, 0:1024], in_=xr[0][:, 0:1024]), reads=[R("xr", 0)], dom="s_xr0")
                        P.op("dve", lambda e, kc=kc: e.tensor_copy(out=xr[0][:, 0:1024], in_=mixT[:, kc, 1024:2048]),
                             reads=[Rmix(kc, t) for t in range(8, 16)], writes=[R("xr", 0)])
                        P.dma(lambda e, kc=kc: e.dma_start(out=tap_d["mix"][kc, :, 1024:2048], in_=xr[0][:, 0:1024]), reads=[R("xr", 0)], dom="s_xr0")
                if "o" in phases:
                    phase_o(s, l)
                if "f" in phases:
                    phase_f(s, l)
        doms = ["pe", "act", "dve", "pool"]
        for o in P.ops:
            if o.dom not in doms:
                doms.append(o.dom)
        sems = {d: es.enter_context(nc.semaphore(d)) for d in doms}
        block = es.enter_context(nc.Block())
        P.emit(block, sems)
        nc._prog_stats = (len(P.ops), len(doms), getattr(P, 'est_ns', None))
    return nc


def _prep_shared(w_in, gla_gate_w2, gla_gate_b, gla_norm_g, nat_rpb, w_out, ln1_g, ln1_b, w_ffn_in, w_ffn_out, ln2_g, ln2_b):
    f = lambda a: np.ascontiguousarray(np.asarray(a, dtype=np.float32))
    w2g = np.zeros((NL, 33, 512), np.float32)
    w2g[:, 0:16, 0:256] = gla_gate_w2[:, 0]
    w2g[:, 16:32, 256:512] = gla_gate_w2[:, 1]
    w2g[:, 32, 0:256] = gla_gate_b[:, 0]
    w2g[:, 32, 256:512] = gla_gate_b[:, 1]
    gn = np.ascontiguousarray(np.broadcast_to(np.asarray(gla_norm_g, np.float32)[:, None, :], (NL, 128, 512)))
    lnp = np.stack([ln1_g, ln1_b, ln2_g, ln2_b], axis=1).astype(np.float32)
    lnp = np.ascontiguousarray(np.broadcast_to(lnp[:, :, None, :], (NL, 4, 128, D)))
    ro, co, valid = _nat_index_tables()
    rpb = np.asarray(nat_rpb, np.float32)
    g = rpb[:, :, ro, co]
    g = g.reshape(NL, 4, 2, 128, 19, 64).transpose(0, 1, 3, 2, 4, 5)
    natb = np.ascontiguousarray(g.reshape(NL, 4, 128, 2 * 19 * 64))
    natm = np.where(valid, np.float32(0.0), np.float32(MASKV)).astype(np.float32).reshape(128, 19 * 64)
    return {
        "w_in": f(w_in), "w_out": f(w_out), "w1": f(w_ffn_in), "w2": f(w_ffn_out),
        "w2g": w2g, "gn": gn, "lnp": lnp, "natb": natb, "natm": np.ascontiguousarray(natm), "cst": _host_consts(),
    }


_NC_CACHE = {}


def kernel(x_prompt, x_sample, w_in, gla_gate_w2, gla_gate_b, gla_norm_g, nat_rpb, w_out,
           ln1_g, ln1_b, w_ffn_in, w_ffn_out, ln2_g, ln2_b):
    xp = np.asarray(x_prompt, np.float32)
    xsm = np.asarray(x_sample, np.float32)
    shared = _prep_shared(np.asarray(w_in), np.asarray(gla_gate_w2), np.asarray(gla_gate_b), np.asarray(gla_norm_g),
                          np.asarray(nat_rpb), np.asarray(w_out), np.asarray(ln1_g), np.asarray(ln1_b),
                          np.asarray(w_ffn_in), np.asarray(w_ffn_out), np.asarray(ln2_g), np.asarray(ln2_b))
    in_maps = []
    for i in range(NCORES):
        xi = np.ascontiguousarray(np.concatenate([xp[4 * i:4 * i + 4], xsm[2 * i:2 * i + 2]], axis=0))
        m = {"x": xi}
        m.update(shared)
        in_maps.append(m)
    if "nc" not in _NC_CACHE:
        _NC_CACHE["nc"] = build()
    res = run_bass_kernel_spmd(_NC_CACHE["nc"], in_maps, core_ids=list(range(NCORES)))
    ys = [np.asarray(r["y"]) for r in res.results]
    y_prompt = np.concatenate([y[0:4] for y in ys], axis=0).astype(np.float32)
    y_sample = np.concatenate([y[4:6] for y in ys], axis=0).astype(np.float32)
    return (y_prompt, y_sample)
```
